# Optimizing a Trainium2 kernel written in Bass

```python
import jax, jax.numpy as jnp
from jax import lax
import numpy as np

D_MODEL = 1024
BATCH = 4
SEQ = 4096
DEPTH = 1

PLE_DIM = 256
RET_HEADS = 4
RET_QK_DIM = 256
RET_V_DIM = 512
RET_CHUNK = 128
MLA_HEADS = 8
MLA_NOPE_DIM = 128
MLA_ROPE_DIM = 64
MLA_QK_DIM = MLA_NOPE_DIM + MLA_ROPE_DIM
MLA_V_DIM = 128
MLA_Q_LORA = 384
MLA_KV_LORA = 256
ATTN_BLOCK = 128
D_FF = 4 * D_MODEL
ROPE_BASE = 10000.0
RMS_EPS = 1e-6

RET_QK_W = RET_HEADS * RET_QK_DIM
RET_V_W = RET_HEADS * RET_V_DIM
MLA_V_W = MLA_HEADS * MLA_V_DIM
IN_SIZES = (RET_QK_W, RET_QK_W, RET_V_W, RET_V_W, MLA_Q_LORA, MLA_KV_LORA, MLA_ROPE_DIM, D_MODEL, D_MODEL)
IN_WIDTH = sum(IN_SIZES)

kernel_name = "hybrid_retention_mla_gated_block"


def rms_norm(x, gain):
    xf = x.astype(jnp.float32)
    y = xf * lax.rsqrt(jnp.mean(xf * xf, axis=-1, keepdims=True) + RMS_EPS)
    return (y * gain.astype(jnp.float32)).astype(x.dtype)


def rope(x, pos):
    half = x.shape[-1] // 2
    inv = ROPE_BASE ** (-jnp.arange(half, dtype=jnp.float32) / half)
    ang = pos.astype(jnp.float32)[:, :, None] * inv
    cos = jnp.cos(ang)[:, :, None, :]
    sin = jnp.sin(ang)[:, :, None, :]
    xf = x.astype(jnp.float32)
    x1, x2 = xf[..., :half], xf[..., half:]
    return jnp.concatenate([x1 * cos - x2 * sin, x2 * cos + x1 * sin], axis=-1).astype(x.dtype)


def retention(q, k, v):
    B, S, H, dk = q.shape
    dv = v.shape[-1]
    C = RET_CHUNK
    N = S // C
    log_g = jnp.log1p(-jnp.exp2(-5.0 - jnp.arange(H, dtype=jnp.float32)))
    idx = jnp.arange(C, dtype=jnp.float32)
    diff = idx[:, None] - idx[None, :]
    decay_in = jnp.where(diff >= 0, jnp.exp(jnp.maximum(diff, 0.0)[None] * log_g[:, None, None]), 0.0)
    q_dec = jnp.exp((idx + 1.0)[None, :] * log_g[:, None])
    k_dec = jnp.exp((C - 1.0 - idx)[None, :] * log_g[:, None])
    chunk_dec = jnp.exp(C * log_g)

    def to_chunks(t):
        return t.astype(jnp.float32).reshape(B, N, C, H, t.shape[-1]).transpose(1, 0, 3, 2, 4)

    def step(state, inp):
        qc, kc, vc = inp
        scores = jnp.einsum('bhnd,bhmd->bhnm', qc, kc) * decay_in
        o = (jnp.einsum('bhnm,bhmv->bhnv', scores, vc)
             + jnp.einsum('bhnd,bhdv->bhnv', qc, state) * q_dec[None, :, :, None])
        state = (state * chunk_dec[None, :, None, None]
                 + jnp.einsum('bhmd,bhmv->bhdv', kc * k_dec[None, :, :, None], vc))
        return state, o

    state0 = jnp.zeros((B, H, dk, dv), jnp.float32)
    _, o = lax.scan(step, state0, (to_chunks(q), to_chunks(k), to_chunks(v)))
    return o.transpose(1, 0, 3, 2, 4).reshape(B, S, H, dv).astype(v.dtype)


def causal_block_attention(q, k, v):
    B, S, H, D = q.shape
    nb = S // ATTN_BLOCK
    scale = D ** -0.5
    qb = q.reshape(B, nb, ATTN_BLOCK, H, D).transpose(1, 0, 2, 3, 4)
    key_pos = jnp.arange(S)

    def one(args):
        qi, bi = args
        s = jnp.einsum('bqhd,bkhd->bhqk', qi, k).astype(jnp.float32) * scale
        q_pos = bi * ATTN_BLOCK + jnp.arange(ATTN_BLOCK)
        s = jnp.where(key_pos[None, :] <= q_pos[:, None], s, -1e30)
        pr = jax.nn.softmax(s, axis=-1).astype(v.dtype)
        return jnp.einsum('bhqk,bkhv->bqhv', pr, v)

    o = lax.map(one, (qb, jnp.arange(nb)))
    return o.transpose(1, 0, 2, 3, 4).reshape(B, S, H, v.shape[-1])


def hybrid_layer(x, p_l, pos, norm_mix, w_in, ret_norm, q_lat_norm, kv_lat_norm, w_uq, w_ukv,
                 q_norm, k_norm, w_br, w_bm, w_o, norm_mlp, w_up, w_down, norm_ple, w_ple_gate, w_ple):
    B, S, _ = x.shape
    h = rms_norm(x, norm_mix)
    proj = h @ w_in
    offsets = np.cumsum(IN_SIZES)[:-1].tolist()
    rq, rk, rv, rg, cq, ckv, krope, gr, gm = jnp.split(proj, offsets, axis=-1)

    rq = rope(rq.reshape(B, S, RET_HEADS, RET_QK_DIM), pos)
    rk = rope(rk.reshape(B, S, RET_HEADS, RET_QK_DIM), pos) * (RET_QK_DIM ** -0.5)
    rv = rv.reshape(B, S, RET_HEADS, RET_V_DIM)
    ro = retention(rq, rk, rv)
    ro = rms_norm(ro, ret_norm.reshape(RET_HEADS, RET_V_DIM)).reshape(B, S, RET_V_W)
    a_ret = (ro * jax.nn.silu(rg)) @ w_br

    cq = rms_norm(cq, q_lat_norm)
    q = (cq @ w_uq).reshape(B, S, MLA_HEADS, MLA_QK_DIM)
    ckv = rms_norm(ckv, kv_lat_norm)
    kv = (ckv @ w_ukv).reshape(B, S, MLA_HEADS, MLA_NOPE_DIM + MLA_V_DIM)
    k_nope, v = kv[..., :MLA_NOPE_DIM], kv[..., MLA_NOPE_DIM:]
    k_rope = jnp.broadcast_to(krope[:, :, None, :], (B, S, MLA_HEADS, MLA_ROPE_DIM))
    k = jnp.concatenate([k_nope, k_rope], axis=-1)
    q = rms_norm(q, q_norm)
    k = rms_norm(k, k_norm)
    q = jnp.concatenate([q[..., :MLA_NOPE_DIM], rope(q[..., MLA_NOPE_DIM:], pos)], axis=-1)
    k = jnp.concatenate([k[..., :MLA_NOPE_DIM], rope(k[..., MLA_NOPE_DIM:], pos)], axis=-1)
    mo = causal_block_attention(q, k, v).reshape(B, S, MLA_V_W)
    a_mla = mo @ w_bm

    mixed = jax.nn.sigmoid(gr) * a_ret + jax.nn.sigmoid(gm) * a_mla
    x = x + mixed @ w_o

    h2 = rms_norm(x, norm_mlp)
    x = x + jnp.square(jax.nn.relu(h2 @ w_up)) @ w_down

    gate = jax.nn.sigmoid(rms_norm(x, norm_ple) @ w_ple_gate)
    x = x + gate * (p_l @ w_ple)
    return x


def setup_inputs(seed: int = 0) -> dict:
    key = jax.random.key(seed)
    ks = jax.random.split(key, 24)

    def dense(k, shape, fan_in):
        return jax.random.normal(k, shape, jnp.float32) * (fan_in ** -0.5)

    def gain(k, n):
        return 1.0 + 0.05 * jax.random.normal(k, (DEPTH, n), jnp.float32)

    x = jax.random.normal(ks[0], (BATCH, SEQ, D_MODEL), jnp.float32)
    p = jax.random.normal(ks[1], (DEPTH, BATCH, SEQ, PLE_DIM), jnp.float32)
    offset = jax.random.randint(ks[2], (BATCH, 1), 0, 1024, dtype=jnp.int32)
    positions = offset + jnp.arange(SEQ, dtype=jnp.int32)[None, :]
    return {
        "x": x,
        "p": p,
        "positions": positions,
        "norm_mix": gain(ks[3], D_MODEL),
        "w_in": dense(ks[4], (DEPTH, D_MODEL, IN_WIDTH), D_MODEL),
        "ret_norm": gain(ks[5], RET_V_W),
        "q_lat_norm": gain(ks[6], MLA_Q_LORA),
        "kv_lat_norm": gain(ks[7], MLA_KV_LORA),
        "w_uq": dense(ks[8], (DEPTH, MLA_Q_LORA, MLA_HEADS * MLA_QK_DIM), MLA_Q_LORA),
        "w_ukv": dense(ks[9], (DEPTH, MLA_KV_LORA, MLA_HEADS * (MLA_NOPE_DIM + MLA_V_DIM)), MLA_KV_LORA),
        "q_norm": gain(ks[10], MLA_QK_DIM),
        "k_norm": gain(ks[11], MLA_QK_DIM),
        "w_br": dense(ks[12], (DEPTH, RET_V_W, D_MODEL), RET_V_W),
        "w_bm": dense(ks[13], (DEPTH, MLA_V_W, D_MODEL), MLA_V_W),
        "w_o": dense(ks[14], (DEPTH, D_MODEL, D_MODEL), D_MODEL),
        "norm_mlp": gain(ks[15], D_MODEL),
        "w_up": dense(ks[16], (DEPTH, D_MODEL, D_FF), D_MODEL),
        "w_down": dense(ks[17], (DEPTH, D_FF, D_MODEL), D_FF),
        "norm_ple": gain(ks[18], D_MODEL),
        "w_ple_gate": dense(ks[19], (DEPTH, D_MODEL, D_MODEL), D_MODEL),
        "w_ple": dense(ks[20], (DEPTH, PLE_DIM, D_MODEL), PLE_DIM),
    }


def reference(x, p, positions, norm_mix, w_in, ret_norm, q_lat_norm, kv_lat_norm, w_uq, w_ukv,
              q_norm, k_norm, w_br, w_bm, w_o, norm_mlp, w_up, w_down, norm_ple, w_ple_gate, w_ple):
    for i in range(DEPTH):
        x = hybrid_layer(x, p[i], positions, norm_mix[i], w_in[i], ret_norm[i], q_lat_norm[i],
                         kv_lat_norm[i], w_uq[i], w_ukv[i], q_norm[i], k_norm[i], w_br[i], w_bm[i],
                         w_o[i], norm_mlp[i], w_up[i], w_down[i], norm_ple[i], w_ple_gate[i], w_ple[i])
    return x
```

```python
import contextlib
import math

import numpy as np
import concourse.bass as bass
import concourse.mybir as mybir
from concourse.bass_utils import run_bass_kernel_spmd

F32 = mybir.dt.float32
BF16 = mybir.dt.bfloat16
I32 = mybir.dt.int32
AF = mybir.ActivationFunctionType
ALU = mybir.AluOpType
AX = mybir.AxisListType

D = 1024
NB = 32
NOWN = 16
EPS = 1e-6
OFF_RQ, OFF_RK, OFF_RV, OFF_RG, OFF_CQ, OFF_CKV, OFF_KR, OFF_GR, OFF_GM = (
    0, 1024, 2048, 4096, 6144, 6528, 6784, 6848, 7872)
NGRP = 4
TWO_PI = 2.0 * math.pi
C1 = 6.28125
C2 = TWO_PI - C1


class Dep:
    __slots__ = ("w", "r")

    def __init__(self):
        self.w = None
        self.r = []


class T:
    __slots__ = ("ap", "d")

    def __init__(self, ap, d=None):
        self.ap = ap
        self.d = d if d is not None else Dep()

    def __getitem__(self, k):
        return self.ap[k]


class Sched:
    ENGS = ("pe", "act", "dve", "pool", "sp")

    def __init__(self, nc, es, n_dma_sems=8):
        self.nc = nc
        self.ops = {e: [] for e in self.ENGS}
        self.cnt = {e: 0 for e in self.ENGS}
        self.waited = {e: {} for e in self.ENGS}
        self.semobj = {}
        for e in self.ENGS:
            self.semobj[("e", e)] = es.enter_context(nc.semaphore("sem_" + e))
        self.dq = {}
        for q in ("sp", "pool"):
            n = n_dma_sems
            for i in range(n):
                self.semobj[("d", q, i)] = es.enter_context(nc.semaphore("dq_%s_%d" % (q, i)))
            self.dq[q] = {"n": [0] * n, "next": 0}

    def _need(self, eng, ev, waits):
        if ev is None:
            return
        key, val = ev
        if eng == "pe" and key == ("e", "pe"):
            return
        if self.waited[eng].get(key, 0) >= val:
            return
        self.waited[eng][key] = val
        waits.append((key, val))

    def _deps(self, eng, reads, writes):
        waits = []
        for d in reads:
            self._need(eng, d.w, waits)
        for d in writes:
            self._need(eng, d.w, waits)
            for ev in d.r:
                self._need(eng, ev, waits)
        return waits

    @staticmethod
    def _commit(ev, reads, writes):
        for d in reads:
            d.r.append(ev)
        for d in writes:
            d.w = ev
            d.r = []

    def op(self, eng, fn, reads=(), writes=()):
        reads = [t.d if isinstance(t, T) else t for t in reads]
        writes = [t.d if isinstance(t, T) else t for t in writes]
        waits = self._deps(eng, reads, writes)
        self.cnt[eng] += 1
        ev = (("e", eng), self.cnt[eng])
        self.ops[eng].append((waits, fn, (("e", eng), 1)))
        self._commit(ev, reads, writes)

    def dma(self, q, out, in_, reads=(), writes=()):
        reads = [t.d if isinstance(t, T) else t for t in reads]
        writes = [t.d if isinstance(t, T) else t for t in writes]
        d = self.dq[q]
        i = d["next"]
        d["next"] = (i + 1) % len(d["n"])
        key = ("d", q, i)
        waits = self._deps(q, reads, writes)
        if d["n"][i]:
            self._need(q, (key, 16 * d["n"][i]), waits)
        d["n"][i] += 1
        ev = (key, 16 * d["n"][i])
        self.ops[q].append((waits, lambda e, o=out, s=in_: e.dma_start(out=o, in_=s), (key, 16)))
        self._commit(ev, reads, writes)

    def all_events(self):
        evs = [(("e", e), self.cnt[e]) for e in self.ENGS if self.cnt[e]]
        for q, d in self.dq.items():
            evs += [(("d", q, i), 16 * n) for i, n in enumerate(d["n"]) if n]
        return evs

    def barrier(self, engines=None):
        evs = self.all_events()
        for e in (engines or self.ENGS):
            waits = []
            for ev in evs:
                self._need(e, ev, waits)
            if waits:
                self.ops[e].append((waits, None, None))

    def emit(self):
        nc = self.nc
        so = self.semobj

        def run(e, name):
            for waits, fn, inc in self.ops[name]:
                for key, val in waits:
                    e.wait_ge(so[key], val)
                if fn is not None:
                    fn(e).then_inc(so[inc[0]], inc[1])

        with nc.Block() as block:
            @block.tensor
            def _(e):
                run(e, "pe")

            @block.scalar
            def _(e):
                run(e, "act")

            @block.vector
            def _(e):
                run(e, "dve")

            @block.gpsimd
            def _(e):
                run(e, "pool")

            @block.sync
            def _(e):
                run(e, "sp")


def build_program(stage=99, debug=False):
    nc = bass.Bass("TRN2", target_bir_lowering=False)

    def din(name, shape, dt=F32):
        return nc.dram_tensor(name, list(shape), dt, kind="ExternalInput").ap()

    xo = din("xo", [NOWN, 128, D])
    xt = din("xt", [NOWN, 128, D])
    pp = din("pp", [NOWN, 128, 256])
    post = din("post", [128, NB], I32)
    kbias_d = din("kbias", [128, NB])
    w_in = din("w_in", [D, 8896])
    w_uq = din("w_uq", [384, 1536])
    w_ukv = din("w_ukv", [256, 2048])
    w_br = din("w_br", [2048, D])
    w_bm = din("w_bm", [D, D])
    w_o = din("w_o", [D, D])
    w_up = din("w_up", [D, 4096])
    w_down = din("w_down", [4096, D])
    w_pg = din("w_pg", [D, D])
    w_ple = din("w_ple", [256, D])
    g_mix = din("g_mix", [1, D])
    g_ret = din("g_ret", [1, 2048])
    g_ql = din("g_ql", [1, 384])
    g_kl = din("g_kl", [1, 256])
    g_q = din("g_q", [1, 192])
    g_k = din("g_k", [1, 192])
    g_mlp = din("g_mlp", [1, D])
    g_ple = din("g_ple", [1, D])
    c_ident = din("c_ident", [128, 128])
    c_tri = din("c_tri", [128, 128])
    c_invm = din("c_invm", [1, 32])
    c_invr = din("c_invr", [1, 128])
    c_dec = din("c_dec", [128, 12])
    out_d = nc.dram_tensor("out", [NOWN, 128, D], F32, kind="ExternalOutput").ap()
    dbg = {}

    def dbg_out(name, shape, dt=F32):
        dbg[name] = nc.dram_tensor(name, list(shape), dt, kind="ExternalOutput").ap()
        return dbg[name]

    with contextlib.ExitStack() as es:
        S = Sched(nc, es)
        ARENA_COLS = 106300
        arena = es.enter_context(nc.sbuf_tensor("arena", [128, ARENA_COLS], BF16))
        banks = [T(es.enter_context(nc.psum_tensor("bank%d" % i, [128, 512], F32))[:]) for i in range(8)]

        class Alloc:
            def __init__(self):
                self.top = 0
                self.marks = []

            def view(self, shape, dt, parts=128):
                n = 1
                for s in shape[1:]:
                    n *= s
                cols = n if dt == BF16 else 2 * n
                cols = (cols + 15) // 16 * 16
                assert self.top + cols <= ARENA_COLS, ("SBUF arena overflow", self.top, cols, shape)
                ap = arena[0:shape[0], self.top:self.top + (n if dt == BF16 else 2 * n)]
                self.top += cols
                if dt != BF16:
                    ap = ap.bitcast(dt)
                if len(shape) == 3:
                    ap = ap.rearrange("p (a b) -> p a b", a=shape[1])
                elif len(shape) == 4:
                    ap = ap.rearrange("p (a b c) -> p a b c", a=shape[1], b=shape[2])
                return ap

            def tile(self, shape, dt):
                return T(self.view(shape, dt))

            def push(self):
                self.marks.append(self.top)

            def pop(self):
                self.top = self.marks.pop()

        A = Alloc()

        def act(out, in_, func, r, w, **kw):
            S.op("act", lambda e: e.activation(out=out, in_=in_, func=func, **kw), r, w)

        def tt(out, in0, in1, op, r, w, eng="dve"):
            S.op(eng, lambda e: e.tensor_tensor(out=out, in0=in0, in1=in1, op=op), r, w)

        def ts(out, in0, s1, s2, op0, op1, r, w, eng="dve"):
            if s2 is None:
                S.op(eng, lambda e: e.tensor_scalar(out=out, in0=in0, scalar1=s1, scalar2=None, op0=op0), r, w)
            else:
                S.op(eng, lambda e: e.tensor_scalar(out=out, in0=in0, scalar1=s1, scalar2=s2, op0=op0, op1=op1), r, w)

        def stt(out, in0, scalar, in1, op0, op1, r, w, eng="dve"):
            S.op(eng, lambda e: e.scalar_tensor_tensor(out=out, in0=in0, scalar=scalar, in1=in1, op0=op0, op1=op1), r, w)

        def cp(eng, out, in_, r, w):
            if eng == "act":
                S.op("act", lambda e: e.copy(out=out, in_=in_), r, w)
            else:
                S.op(eng, lambda e: e.tensor_copy(out=out, in_=in_), r, w)

        def mm(out, lhsT, rhs, start, stop, r, w):
            S.op("pe", lambda e: e.matmul(out, lhsT=lhsT, rhs=rhs, start=start, stop=stop), r, w)

        def memset(ap, val, w, eng="dve"):
            S.op(eng, lambda e: e.memset(ap, val), [], w)

        cpi = [0]
        cp_pref = ["alt"]

        def cp_alt(out, in_, r, w):
            cpi[0] += 1
            if cp_pref[0] == "alt":
                cp("act" if cpi[0] % 2 else "dve", out, in_, r, w)
            else:
                cp(cp_pref[0], out, in_, r, w)

        ident = A.tile([128, 128], BF16)
        tri = A.tile([128, 128], BF16)
        ones = A.tile([128, 128], BF16)
        epsc = A.tile([128, 1], F32)
        post_i = A.tile([128, NB], I32)
        post_f = A.tile([128, NB], F32)
        kbias = A.tile([128, NB], F32)
        dec = A.tile([128, 12], F32)
        S.dma("pool", ident[:], c_ident, [], [ident])
        S.dma("pool", tri[:], c_tri, [], [tri])
        S.dma("sp", post_i[:], post, [], [post_i])
        S.dma("sp", kbias[:], kbias_d, [], [kbias])
        S.dma("sp", dec[:], c_dec, [], [dec])
        memset(ones[:], 1.0, [ones])
        memset(epsc[:], EPS, [epsc])
        onec = A.tile([128, 1], F32)
        memset(onec[:], 1.0, [onec])
        cp("dve", post_f[:], post_i[:], [post_i], [post_f])

        trn = [0]
        tbanks = [6, 7]

        def transpose_to(dst_ap, src_ap, ncols_in, nblk, r_src, w_dst, in_parts=128, evac="alt"):
            bk = banks[tbanks[trn[0] % len(tbanks)]]
            trn[0] += 1
            pv = bk.ap.bitcast(BF16).rearrange("p (a b) -> p a b", a=8)
            for i in range(nblk):
                S.op("pe", lambda e, i=i: e.transpose(out=pv[0:ncols_in, i, :], in_=src_ap[i], identity=ident[:]),
                     list(r_src) + [ident], [bk])
            if evac == "alt":
                cp_alt(dst_ap, pv[0:ncols_in, 0:nblk, :], [bk], w_dst)
            else:
                cp(evac, dst_ap, pv[0:ncols_in, 0:nblk, :], [bk], w_dst)

        def rstd_from_ms(rstd_ap, ms_ap, r, w):
            act(rstd_ap, ms_ap, AF.Ln, r + [epsc], w, bias=epsc[:, 0:1], scale=1.0)
            act(rstd_ap, rstd_ap, AF.Exp, w, w, scale=-0.5)

        def sincos(cs, kf, ki, r, n, do_sin=True):
            ts(cs[:, n:2 * n], cs[:, n:2 * n], 0.5 * math.pi, None, ALU.add, None, [cs], [cs])
            ts(kf[:], cs[:], 1.0 / TWO_PI, None, ALU.mult, None, [cs], [kf])
            cp("dve", ki[:], kf[:], [kf], [ki])
            cp("dve", kf[:], ki[:], [ki], [kf])
            stt(cs[:], kf[:], -C1, cs[:], ALU.mult, ALU.add, [kf, cs], [cs])
            stt(cs[:], kf[:], -C2, cs[:], ALU.mult, ALU.add, [kf, cs], [cs])
            ts(kf[:], cs[:], math.pi, -TWO_PI, ALU.is_gt, ALU.mult, [cs], [kf])
            tt(cs[:], cs[:], kf[:], ALU.add, [cs, kf], [cs])
            if do_sin:
                act(cs[:], cs[:], AF.Sin, [cs], [cs])

        def bload(dst, src_row, eng="sp"):
            S.dma(eng, dst[:], src_row.partition_broadcast(128), [], [dst])

        NSLOT = 3
        OT = A.view([128, 8, NOWN * 128], BF16)
        OT_d = [Dep() for _ in range(NGRP)]
        A.push()

        ckvT = A.view([128, 2, NB * 128], BF16)
        ckvT_d = [Dep() for _ in range(NB)]
        KTr = A.view([128, NB * 128], BF16)
        KTr_z = Dep()
        S.op("dve", lambda e: e.memset(KTr[64:128, :], 0.0), [], [KTr_z])
        KTr_d = [Dep() for _ in range(NB)]
        cqT = A.view([128, 3, NOWN * 128], BF16)
        cqT_d = [Dep() for _ in range(NOWN)]
        ssr = A.tile([128, NB], F32)
        csM = A.tile([128, 2 * NB * 32], F32)
        sinM = csM.ap[:, 0:NB * 32].rearrange("p (a b) -> p a b", a=NB)
        cosM = csM.ap[:, NB * 32:2 * NB * 32].rearrange("p (a b) -> p a b", a=NB)
        G_k = A.tile([128, 192], F32)
        G_q = A.tile([128, 192], F32)
        gk_col = A.tile([128, 1], F32)
        bload(G_k, g_k)
        bload(G_q, g_q)
        S.dma("sp", gk_col[:], g_k[0:1, 0:128].rearrange("o d -> d o"), [], [gk_col])

        A.push()
        G_mix = A.tile([128, D], F32)
        G_ql = A.tile([128, 384], F32)
        G_kl = A.tile([128, 256], F32)
        invM = A.tile([128, 32], F32)
        bload(G_mix, g_mix)
        bload(G_ql, g_ql)
        bload(G_kl, g_kl)
        bload(invM, c_invm)
        WA = A.tile([128, 8, 320], BF16)
        WB = A.tile([128, 8, 384], BF16)
        w_in_v = w_in.rearrange("(c p) n -> p c n", p=128)
        S.dma("pool", WA[:], w_in_v[:, :, OFF_CKV:OFF_CKV + 320], [], [WA])
        S.dma("pool", WB[:], w_in_v[:, :, OFF_CQ:OFF_CQ + 384], [], [WB])
        kfM = A.tile([128, 2 * NB * 32], F32)
        kiM = A.tile([128, 2 * NB * 32], I32)
        for half in range(2):
            o3 = csM.ap[:, half * NB * 32:(half + 1) * NB * 32].rearrange("p (a b) -> p a b", a=NB)
            tt(o3, post_f[:].unsqueeze(2).to_broadcast([128, NB, 32]),
               invM[:].unsqueeze(1).to_broadcast([128, NB, 32]), ALU.mult, [post_f, invM], [csM])
        sincos(csM, kfM, kiM, [], NB * 32)

        junk = A.tile([128, D], BF16)

        def norm_batch_gen(items, G, st, c0):
            n = len(items)
            for k, (xb, xn_, xT, src) in enumerate(items):
                if src is not None:
                    S.dma("sp", xb[:], src, [], [xb])
                act(junk[:], xb[:], AF.Square, [xb], [junk, st], scale=1.0 / 32.0, accum_out=st[:, c0 + k:c0 + k + 1])
                yield
            rstd_from_ms(st[:, c0:c0 + n], st[:, c0:c0 + n], [st], [st])
            yield
            for k, (xb, xn_, xT, src) in enumerate(items):
                stt(xn_[:], xb[:], st[:, c0 + k:c0 + k + 1], G[:], ALU.mult, ALU.mult, [xb, st, G], [xn_])
                yield
            for k, (xb, xn_, xT, src) in enumerate(items):
                transpose_to(xT[:], [xn_[:, c * 128:(c + 1) * 128] for c in range(8)], 128, 8, [xn_], [xT])
                yield

        def norm_batch(items, G, st, c0):
            for _ in norm_batch_gen(items, G, st, c0):
                pass

        NBT = 4
        xb1 = [[A.tile([128, D], F32) for _ in range(NBT)] for _ in range(2)]
        xn1 = [[A.tile([128, D], BF16) for _ in range(NBT)] for _ in range(2)]
        xT1 = [[A.tile([128, 8, 128], BF16) for _ in range(NBT)] for _ in range(2)]
        st1 = [A.tile([128, 16], F32) for _ in range(2)]
        for t_ in st1:
            memset(t_[:], 1.0, [t_])
        ckvn = [A.tile([128, 256], BF16) for _ in range(NBT)]
        cqn = [A.tile([128, 384], BF16) for _ in range(NBT)]
        krg = [A.tile([128, 64], F32) for _ in range(NBT)]
        krA = [A.tile([128, 64], F32) for _ in range(NBT)]
        krB = [A.tile([128, 64], F32) for _ in range(NBT)]
        krr = [A.tile([128, 64], BF16) for _ in range(NBT)]

        nb1 = NB if stage >= 1 else 0

        def p1_norm(p0):
                par = (p0 // NBT) % 2
                st = st1[par]
                ps = list(range(p0, p0 + NBT))
                yield from norm_batch_gen([(xb1[par][k], xn1[par][k], xT1[par][k], (xo if p % 2 else xt)[p // 2])
                                       for k, p in enumerate(ps)], G_mix, st, 0)

        def p1_latent(p0):
                par = (p0 // NBT) % 2
                st = st1[par]
                ps = list(range(p0, p0 + NBT))
                bkB_of = {}
                for k, p in enumerate(ps):
                    xT = xT1[par][k]
                    bkA = banks[k]
                    for c in range(8):
                        mm(bkA[:, 0:320], xT[:, c, :], WA[:, c, :], c == 0, c == 7, [xT, WA], [bkA])
                    if p % 2 == 1:
                        bkB = banks[4 + (k // 2) % 2]
                        bkB_of[k] = bkB
                        for c in range(8):
                            mm(bkB[:, 0:384], xT[:, c, :], WB[:, c, :], c == 0, c == 7, [xT, WB], [bkB])
                    yield
                for k, p in enumerate(ps):
                    bkA = banks[k]
                    act(junk[:, 0:256], bkA[:, 0:256], AF.Square, [bkA], [junk, st], scale=1.0 / 16.0,
                        accum_out=st[:, 4 + k:5 + k])
                    act(junk[:, 256:320], bkA[:, 256:320], AF.Square, [bkA], [junk, ssr], accum_out=ssr[:, p:p + 1])
                    if p % 2 == 1:
                        act(junk[:, 0:384], bkB_of[k][:, 0:384], AF.Square, [bkB_of[k]], [junk, st],
                            scale=1.0 / math.sqrt(384.0), accum_out=st[:, 8 + k:9 + k])
                    yield
                rstd_from_ms(st[:, 4:12], st[:, 4:12], [st], [st])
                yield
                for k, p in enumerate(ps):
                    bkA = banks[k]
                    cn = ckvn[k]
                    stt(cn[:], bkA[:, 0:256], st[:, 4 + k:5 + k], G_kl[:], ALU.mult, ALU.mult, [bkA, st, G_kl], [cn])
                    kg, kA, kB, kr = krg[k], krA[k], krB[k], krr[k]
                    tt(kg[:], bkA[:, 256:320], G_k[:, 128:192], ALU.mult, [bkA, G_k], [kg])
                    kg3 = kg[:].rearrange("p (a b) -> p a b", a=2)
                    tt(kA[:].rearrange("p (a b) -> p a b", a=2), kg3,
                       cosM[:, p, :].unsqueeze(1).to_broadcast([128, 2, 32]), ALU.mult, [kg, csM], [kA])
                    tt(kB[:].rearrange("p (a b) -> p a b", a=2), kg3,
                       sinM[:, p, :].unsqueeze(1).to_broadcast([128, 2, 32]), ALU.mult, [kg, csM], [kB])
                    tt(kr[:, 0:32], kA[:, 0:32], kB[:, 32:64], ALU.subtract, [kA, kB], [kr])
                    tt(kr[:, 32:64], kA[:, 32:64], kB[:, 0:32], ALU.add, [kA, kB], [kr])
                    if p % 2 == 1:
                        qn_ = cqn[k]
                        stt(qn_[:], bkB_of[k][:, 0:384], st[:, 8 + k:9 + k], G_ql[:], ALU.mult, ALU.mult,
                            [bkB_of[k], st, G_ql], [qn_])
                    yield
                for k, p in enumerate(ps):
                    m = p // 2
                    cn, kr = ckvn[k], krr[k]
                    transpose_to(ckvT[:, :, p * 128:(p + 1) * 128], [cn[:, c * 128:(c + 1) * 128] for c in range(2)],
                                 128, 2, [cn], [ckvT_d[p]])
                    transpose_to(KTr[0:64, p * 128:(p + 1) * 128].unsqueeze(1), [kr[:, 0:64]], 64, 1, [kr], [KTr_d[p]])
                    if p % 2 == 1:
                        qn_ = cqn[k]
                        transpose_to(cqT[:, :, m * 128:(m + 1) * 128], [qn_[:, c * 128:(c + 1) * 128] for c in range(3)],
                                     128, 3, [qn_], [cqT_d[m]])
                    yield

        def zip_gens(ga, gb):
            da = db = False
            while not (da and db):
                if not da:
                    try:
                        next(ga)
                    except StopIteration:
                        da = True
                if not db:
                    try:
                        next(gb)
                    except StopIteration:
                        db = True

        p0s = list(range(0, nb1, NBT))
        if p0s:
            for _ in p1_norm(p0s[0]):
                pass
        for n_, p0 in enumerate(p0s):
            nxt = p1_norm(p0s[n_ + 1]) if n_ + 1 < len(p0s) else iter(())
            zip_gens(p1_latent(p0), nxt)
        if debug and stage >= 1:
            S.barrier()
            d1 = dbg_out("d_ckvT", [128, 2, NB * 128], BF16)
            S.dma("sp", d1, ckvT, ckvT_d, [])
            d2 = dbg_out("d_KTr", [64, NB * 128], BF16)
            S.dma("sp", d2, KTr[0:64, :], KTr_d, [])
            d3 = dbg_out("d_cqT", [128, 3, NOWN * 128], BF16)
            S.dma("sp", d3, cqT, cqT_d, [])
            d4 = dbg_out("d_ssr", [128, NB])
            S.dma("sp", d4, ssr[:], [ssr], [])
        A.pop()
        S.barrier()

        A.push()
        if stage >= 2:
            Wkv = A.tile([128, 2, 2048], BF16)
            Wq = A.tile([128, 3, 1536], BF16)
            S.dma("pool", Wkv[:], w_ukv.rearrange("(c p) n -> p c n", p=128), [], [Wkv])
            S.dma("pool", Wq[:], w_uq.rearrange("(c p) n -> p c n", p=128), [], [Wq])
            Wkv4 = Wkv[:].rearrange("p c (h t d) -> p c h t d", t=2, d=128)
            KTn = A.view([128, 4, NB * 128], BF16)
            KTn_d = [[Dep() for _ in range(8)] for _ in range(4)]
            Vt = A.view([128, NB, 4, 128], BF16)
            Vt_d = [Dep() for _ in range(NB)]
            ssk = A.tile([128, NB, 4], F32)
            ksc = A.tile([128, NB, 4], F32)
            sq = [A.tile([128, 512], F32) for _ in range(2)]
            QTn = [A.view([128, 4, 512], BF16) for _ in range(2)]
            QTn_d = [[Dep() for _ in range(4)] for _ in range(2)]
            QTr = [A.view([128, 4, 512], BF16) for _ in range(2)]
            QTr_z = Dep()
            for q_ in QTr:
                S.op("dve", lambda e, q_=q_: e.memset(q_[64:128, :, :], 0.0), [], [QTr_z])
            QTr_d = [[Dep() for _ in range(4)] for _ in range(2)]
            qst = [A.tile([128, 8], F32) for _ in range(2)]
            qn32 = [A.tile([128, 4, 192], F32) for _ in range(2)]
            qA = [A.tile([128, 4, 64], F32) for _ in range(2)]
            qB = [A.tile([128, 4, 64], F32) for _ in range(2)]
            qbf = [A.tile([128, 4, 192], BF16) for _ in range(2)]
            NPB = 4
            Pb = [A.tile([128, 512], BF16) for _ in range(NPB)]
            rec = A.tile([128, 512], F32)
            lnsc = A.tile([128, 1], F32)
            memset(lnsc[:], -0.5 * math.log(192.0), [lnsc])
            pbi = 0
            sbi = 0
            tbanks[:] = [7]
            SBK = [0, 1, 6]
            seq = [(h_, g_) for h_ in range(2) for g_ in range(NGRP)]

            def qbuild(hh, g, i, qb):
                m = 4 * g + i
                p_own = 2 * m + 1
                st = qst[i % 2]
                q32, qa, qb_, qh = qn32[i % 2], qA[i % 2], qB[i % 2], qbf[i % 2]
                for half in range(2):
                    bk = banks[(2 * i + half) % 2]
                    c0 = (4 * hh + 2 * half) * 192
                    for c in range(3):
                        mm(bk[:, 0:384], cqT[:, c, m * 128:(m + 1) * 128], Wq[:, c, c0:c0 + 384],
                           c == 0, c == 2, [cqT_d[m], Wq], [bk])
                    s_ = sq[half]
                    act(s_[:, 0:384], bk[:, 0:384], AF.Square, [bk], [s_], scale=1.0 / math.sqrt(192.0))
                    S.op("dve", lambda e, s_=s_, st=st, half=half: e.tensor_reduce(
                        out=st[:, 2 * half:2 * half + 2], in_=s_[:, 0:384].rearrange("p (h d) -> p h d", h=2),
                        axis=AX.X, op=ALU.add), [s_], [st])
                    rstd_from_ms(st[:, 2 * half:2 * half + 2], st[:, 2 * half:2 * half + 2], [st], [st])
                    for hq in range(2):
                        stt(q32[:, 2 * half + hq, :], bk[:, hq * 192:(hq + 1) * 192],
                            st[:, 2 * half + hq:2 * half + hq + 1], G_q[:], ALU.mult, ALU.mult,
                            [bk, st, G_q], [q32])
                qr4 = q32[:, :, 128:192].rearrange("p h (a b) -> p h a b", a=2)
                tt(qa[:].rearrange("p h (a b) -> p h a b", a=2), qr4,
                   cosM[:, p_own, :].unsqueeze(1).unsqueeze(1).to_broadcast([128, 4, 2, 32]), ALU.mult,
                   [q32, csM], [qa])
                tt(qb_[:].rearrange("p h (a b) -> p h a b", a=2), qr4,
                   sinM[:, p_own, :].unsqueeze(1).unsqueeze(1).to_broadcast([128, 4, 2, 32]), ALU.mult,
                   [q32, csM], [qb_])
                tt(qh[:, :, 128:160], qa[:, :, 0:32], qb_[:, :, 32:64], ALU.subtract, [qa, qb_], [qh])
                tt(qh[:, :, 160:192], qa[:, :, 32:64], qb_[:, :, 0:32], ALU.add, [qa, qb_], [qh])
                cp("act", qh[:, :, 0:128], q32[:, :, 0:128], [q32], [qh])
                transpose_to(QTn[qb][:, :, i * 128:(i + 1) * 128], [qh[:, hl, 0:128] for hl in range(4)],
                             128, 4, [qh], [QTn_d[qb][i]])
                transpose_to(QTr[qb][0:64, :, i * 128:(i + 1) * 128], [qh[:, hl, 128:192] for hl in range(4)],
                             64, 4, [qh], [QTr_d[qb][i]])


            def attend(hh, g, hl, qb):
                nonlocal pbi, sbi
                h = 4 * hh + hl
                bO = banks[2 + hl % 2]
                bL = banks[4 + hl % 2]
                plist = list(range(8 * g + 8))
                pend = []

                def finish(item, first, last):
                    p_, i0_, Pt_ = item
                    n0 = i0_ * 128
                    mm(bO[:, n0:512], Vt[:, p_, hl, :], Pt_[:, n0:512], first, last, [Vt_d[p_], Pt_], [bO])
                    mm(bL[:, n0:512], ones[:], Pt_[:, n0:512], first, last, [ones, Pt_], [bL])

                for idx, p in enumerate(plist):
                    r = p - 8 * g
                    i0 = 0 if r <= 1 else r // 2
                    n0 = i0 * 128
                    bS = banks[SBK[sbi % 3]]
                    sbi += 1
                    mm(bS[:, n0:512], KTn[:, hl, p * 128:(p + 1) * 128], QTn[qb][:, hl, n0:512], True, False,
                       [KTn_d[hl][p // 4]] + QTn_d[qb][i0:4], [bS])
                    mm(bS[:, n0:512], KTr[:, p * 128:(p + 1) * 128], QTr[qb][:, hl, n0:512], False, True,
                       [KTr_d[p], KTr_z, QTr_z] + QTr_d[qb][i0:4], [bS])
                    Pt = Pb[pbi % NPB]
                    pbi += 1
                    act(Pt[:, n0:512], bS[:, n0:512], AF.Exp, [bS, ksc, kbias], [Pt],
                        bias=kbias[:, p:p + 1], scale=ksc[:, p, hl:hl + 1])
                    if r >= 1 and r % 2 == 1:
                        tt(Pt[:, n0:n0 + 128], Pt[:, n0:n0 + 128], tri[:], ALU.mult, [Pt, tri], [Pt])
                    pend.append((p, i0, Pt))
                    if len(pend) > 2:
                        it_ = pend.pop(0)
                        finish(it_, it_[0] == 0, False)
                while pend:
                    it_ = pend.pop(0)
                    finish(it_, it_[0] == 0, len(pend) == 0)
                S.op("dve", lambda e, bL=bL: e.reciprocal(out=rec[:], in_=bL[:]), [bL], [rec])
                tt(OT[:, h, g * 512:(g + 1) * 512], bO[:], rec[:], ALU.mult, [bO, rec], [OT_d[g]])

            for hh in range(2):
                def kvA():
                    for p in range(NB):
                        bkv = banks[(2 * p) % 4]
                        bkk = banks[(2 * p + 1) % 4]
                        for c in range(2):
                            mm(bkv[:].rearrange("p (h d) -> p h d", h=4), ckvT[:, c, p * 128:(p + 1) * 128],
                               Wkv4[:, c, 4 * hh:4 * hh + 4, 1, :], c == 0, c == 1, [ckvT_d[p], Wkv], [bkv])
                        cp_alt(Vt[:, p, :, :], bkv[:].rearrange("p (h d) -> p h d", h=4), [bkv], [Vt_d[p]])
                        for c in range(2):
                            mm(bkk[:].rearrange("p (h d) -> p h d", h=4), ckvT[:, c, p * 128:(p + 1) * 128],
                               Wkv4[:, c, 4 * hh:4 * hh + 4, 0, :], c == 0, c == 1, [ckvT_d[p], Wkv], [bkk])
                        s_ = sq[p % 2]
                        act(s_[:], bkk[:], AF.Square, [bkk], [s_])
                        S.op("dve", lambda e, s_=s_, p=p: e.tensor_reduce(
                            out=ssk[:, p, :], in_=s_[:].rearrange("p (h d) -> p h d", h=4), axis=AX.X, op=ALU.add),
                            [s_], [ssk])
                        yield
                def kvB():
                    for hl in range(4):
                        h = 4 * hh + hl
                        for tc in range(8):
                            bk = banks[4 + (hl * 8 + tc) % 2]
                            for c in range(2):
                                mm(bk[:], Wkv[:, c, h * 256:h * 256 + 128], ckvT[:, c, tc * 512:(tc + 1) * 512],
                                   c == 0, c == 1, [Wkv] + ckvT_d[4 * tc:4 * tc + 4], [bk])
                            if (hl * 8 + tc) % 2:
                                act(KTn[:, hl, tc * 512:(tc + 1) * 512], bk[:], AF.Copy, [bk, gk_col], [KTn_d[hl][tc]],
                                    scale=gk_col[:, 0:1])
                            else:
                                ts(KTn[:, hl, tc * 512:(tc + 1) * 512], bk[:], gk_col[:, 0:1], None, ALU.mult, None,
                                   [bk, gk_col], [KTn_d[hl][tc]])
                            yield
                zip_gens(kvA(), kvB())
                tt(ksc[:], ssk[:], ssr[:].unsqueeze(2).to_broadcast([128, NB, 4]), ALU.add, [ssk, ssr], [ksc])
                act(ksc[:], ksc[:], AF.Ln, [ksc, epsc], [ksc], bias=epsc[:, 0:1], scale=1.0 / 192.0)
                act(ksc[:], ksc[:], AF.Exp, [ksc, lnsc], [ksc], bias=lnsc[:, 0:1], scale=-0.5)
                if hh == 0:
                    for i in range(4):
                        qbuild(0, 0, i, 0)
                for g in range(NGRP):
                    si = hh * NGRP + g
                    for hl in range(4):
                        attend(hh, g, hl, si % 2)
                        if si + 1 < len(seq):
                            qbuild(seq[si + 1][0], seq[si + 1][1], hl, (si + 1) % 2)
            if debug:
                S.barrier()
                d5 = dbg_out("d_OT", [128, 8, NOWN * 128], BF16)
                S.dma("sp", d5, OT, OT_d, [])
        A.pop()
        A.pop()
        S.barrier()

        if stage >= 3:
            A.push()
            cp_pref[0] = "act"
            tbanks[:] = [6, 7]
            NOPOOL = ("pe", "act", "dve", "sp")
            Gbuf = A.tile([128, D], F32)
            invR = A.tile([128, 128], F32)
            bload(invR, c_invr)
            Wple = A.tile([128, 2, D], BF16)
            S.dma("pool", Wple[:], w_ple.rearrange("(c p) n -> p c n", p=128), [], [Wple])
            state = [A.tile([128, 512], F32) for _ in range(8)]
            state_bf = [A.tile([128, 512], BF16) for _ in range(2)]
            for s_ in state:
                memset(s_[:], 0.0, [s_])
            slots = [A.tile([128, 8, 512], BF16) for _ in range(NSLOT)]
            sl_i = [0]

            def wslot(w_ap, k0, nk, c0, ncols=512):
                sl = slots[sl_i[0] % NSLOT]
                sl_i[0] += 1
                src = w_ap[k0 * 128:(k0 + nk) * 128, c0:c0 + ncols].rearrange("(c p) n -> p c n", p=128)
                S.dma("pool", sl[:, 0:nk, 0:ncols], src, [], [sl])
                return sl

            xown = [A.tile([128, D], F32) for _ in range(4)]
            junk = A.tile([128, D], BF16)
            xnb = [A.tile([128, D], BF16) for _ in range(2)]
            xnTo = [A.tile([128, 8, 128], BF16) for _ in range(4)]
            stt3 = [A.tile([128, 8], F32) for _ in range(2)]
            ost = [A.tile([128, 8], F32) for _ in range(2)]
            roT = A.view([128, 16, 512], BF16)
            roT_d = [Dep() for _ in range(4)]
            pbf = [A.tile([128, 256], BF16) for _ in range(2)]
            gam = [1.0 - 2.0 ** (-5 - h) for h in range(4)]
            gamC = [g_ ** 128 for g_ in gam]

            def proj_tm(w_ap, K, c0, ncols, lhs_fn, blocks, evac, bank_of):
                nkp = max(K // 1024, 1)
                kper = min(K // 128, 8)
                bks = {}
                for kp in range(nkp):
                    sl = wslot(w_ap, kp * 8, kper, c0, ncols)
                    for bi, b in enumerate(blocks):
                        if kp == 0:
                            bks[b] = banks[bank_of(bi)]
                        bk = bks[b]
                        for c in range(kper):
                            lap, ldeps = lhs_fn(b, kp * 8 + c)
                            mm(bk[:, 0:ncols], lap, sl[:, c, 0:ncols], kp == 0 and c == 0,
                               kp == nkp - 1 and c == kper - 1, list(ldeps) + [sl], [bk])
                        if kp == nkp - 1:
                            evac(b, bk)

            def sigmoid_from(bk, ncols, dst_t):
                act(dst_t[:, 0:ncols], bk[:, 0:ncols], AF.Exp, [bk], [dst_t], scale=-1.0)
                act(dst_t[:, 0:ncols], dst_t[:, 0:ncols], AF.Ln, [dst_t, onec], [dst_t], bias=onec[:, 0:1], scale=1.0)
                act(dst_t[:, 0:ncols], dst_t[:, 0:ncols], AF.Exp, [dst_t], [dst_t], scale=-1.0)

            def norm_to_T(b, G_src, dstT, dstT_d):
                st = stt3[b % 2]
                xn_ = xnb[b % 2]
                act(junk[:], xown[b][:], AF.Square, [xown[b]], [junk, st], scale=1.0 / 32.0, accum_out=st[:, 0:1])
                rstd_from_ms(st[:, 0:1], st[:, 0:1], [st], [st])
                stt(xn_[:], xown[b][:], st[:, 0:1], G_src[:], ALU.mult, ALU.mult, [xown[b], st, G_src], [xn_])
                transpose_to(dstT[:, :, b * 128:(b + 1) * 128], [xn_[:, c * 128:(c + 1) * 128] for c in range(8)],
                             128, 8, [xn_], [dstT_d[b]])

            for g in range(NGRP):
                S.barrier(NOPOOL)
                A.push()
                xnTt = [A.tile([128, 8, 128], BF16) for _ in range(4)]
                csR = A.tile([128, 2 * 8 * 128], F32)
                sinR = csR.ap[:, 0:1024].rearrange("p (a b) -> p a b", a=8)
                cosR = csR.ap[:, 1024:2048].rearrange("p (a b) -> p a b", a=8)
                A.push()
                xoth = [A.tile([128, D], F32) for _ in range(4)]
                xn8 = [A.tile([128, D], BF16) for _ in range(8)]
                kfR = A.tile([128, 2048], F32)
                kiR = A.tile([128, 2048], I32)
                bload(Gbuf, g_mix)
                for half in range(2):
                    o3 = csR.ap[:, half * 1024:(half + 1) * 1024].rearrange("p (a b) -> p a b", a=8)
                    tt(o3, post_f[:, 8 * g:8 * g + 8].unsqueeze(2).to_broadcast([128, 8, 128]),
                       invR[:].unsqueeze(1).to_broadcast([128, 8, 128]), ALU.mult, [post_f, invR], [csR])
                sincos(csR, kfR, kiR, [], 1024, do_sin=False)
                items = []
                for i in range(4):
                    items.append((xoth[i], xn8[2 * i], xnTt[i], xt[4 * g + i]))
                    items.append((xown[i], xn8[2 * i + 1], xnTo[i], xo[4 * g + i]))
                norm_batch(items, Gbuf, stt3[0], 0)
                act(csR[:], csR[:], AF.Sin, [csR], [csR])
                A.pop()
                S.barrier(NOPOOL)

                A.push()
                HB = []
                for _ in range(2):
                    HB.append(dict(
                        Kh=[A.tile([128, 256], BF16) for _ in range(8)],
                        Vv=[A.tile([128, 512], BF16) for _ in range(8)],
                        Qt=[A.tile([128, 256], BF16) for _ in range(4)],
                        Kt=[A.tile([128, 256], BF16) for _ in range(4)],
                        SG=[A.tile([128, 512], F32) for _ in range(4)]))
                Gr1 = A.tile([128, 512], F32)
                rA = [A.tile([128, 256], F32)] * 2
                rB = [A.tile([128, 256], F32)] * 2
                rC = [A.tile([128, 256], F32) for _ in range(2)]
                sig = [A.tile([128, 512], F32)] * 2
                QT = [A.tile([128, 2, 128], BF16) for _ in range(2)]
                KT = [A.tile([128, 2, 128], BF16) for _ in range(2)]
                scb = [A.tile([128, 128], BF16) for _ in range(2)]
                rog = [A.tile([128, 512], BF16) for _ in range(2)]

                def lhs_x(b, k):
                    i, own = b
                    t_ = (xnTo if own else xnTt)[i]
                    return t_[:, k, :], [t_]

                all8 = [(i, own) for i in range(4) for own in (0, 1)]
                own4 = [(i, 1) for i in range(4)]

                def proj_gen(h):
                    hb = HB[h % 2]
                    Kh, Vv, Qt, Kt, SG, Gr = hb["Kh"], hb["Vv"], hb["Qt"], hb["Kt"], hb["SG"], Gr1
                    bload(Gr, g_ret[0:1, h * 512:(h + 1) * 512])

                    def rope_evac(kind):
                        def ev(b, bk):
                            i, own = b
                            blk = 2 * i + own
                            a_, b_, c_ = rA[blk % 2], rB[blk % 2], rC[blk % 2]
                            x3 = bk[:, 0:256].rearrange("p (a f) -> p a f", a=2)
                            cb = cosR[:, blk, :].unsqueeze(1).to_broadcast([128, 2, 128])
                            sb_ = sinR[:, blk, :].unsqueeze(1).to_broadcast([128, 2, 128])
                            dcol = dec[:, h:h + 1] if kind == "q" else dec[:, 8 + h:9 + h]
                            stt(a_[:].rearrange("p (a f) -> p a f", a=2), x3, dcol, cb, ALU.mult, ALU.mult,
                                [bk, csR, dec], [a_])
                            stt(b_[:].rearrange("p (a f) -> p a f", a=2), x3, dcol, sb_, ALU.mult, ALU.mult,
                                [bk, csR, dec], [b_])
                            if kind == "q":
                                tt(Qt[i][:, 0:128], a_[:, 0:128], b_[:, 128:256], ALU.subtract, [a_, b_], [Qt[i]])
                                tt(Qt[i][:, 128:256], a_[:, 128:256], b_[:, 0:128], ALU.add, [a_, b_], [Qt[i]])
                            elif not own:
                                tt(Kh[blk][:, 0:128], a_[:, 0:128], b_[:, 128:256], ALU.subtract, [a_, b_], [Kh[blk]])
                                tt(Kh[blk][:, 128:256], a_[:, 128:256], b_[:, 0:128], ALU.add, [a_, b_], [Kh[blk]])
                            else:
                                tt(c_[:, 0:128], a_[:, 0:128], b_[:, 128:256], ALU.subtract, [a_, b_], [c_])
                                tt(c_[:, 128:256], a_[:, 128:256], b_[:, 0:128], ALU.add, [a_, b_], [c_])
                                act(Kh[blk][:], c_[:], AF.Copy, [c_], [Kh[blk]])
                                act(Kt[i][:], c_[:], AF.Copy, [c_], [Kt[i]], scale=float(gam[h] ** -128))
                        return ev

                    def v_evac(b, bk):
                        i, own = b
                        cp_alt(Vv[2 * i + own][:], bk[:], [bk], [Vv[2 * i + own]])

                    def g_evac(b, bk):
                        i, own = b
                        sg = sig[i % 2]
                        sigmoid_from(bk, 512, sg)
                        tt(sg[:], sg[:], bk[:], ALU.mult, [sg, bk], [sg])
                        tt(SG[i][:], sg[:], Gr[:], ALU.mult, [sg, Gr], [SG[i]])

                    for (c0, ncols, blks, ev) in ((OFF_RK + h * 256, 256, all8, rope_evac("k")),
                                                  (OFF_RQ + h * 256, 256, own4, rope_evac("q")),
                                                  (OFF_RV + h * 512, 512, all8, v_evac),
                                                  (OFF_RG + h * 512, 512, own4, g_evac)):
                        sl = wslot(w_in, 0, 8, c0, ncols)
                        for bi, b_ in enumerate(blks):
                            bk = banks[bi % 4]
                            for c in range(8):
                                lap, ldeps = lhs_x(b_, c)
                                mm(bk[:, 0:ncols], lap, sl[:, c, 0:ncols], c == 0, c == 7, list(ldeps) + [sl], [bk])
                            ev(b_, bk)
                            yield

                def ret_gen(h):
                    hb = HB[h % 2]
                    Kh, Vv, Qt, Kt, SG = hb["Kh"], hb["Vv"], hb["Qt"], hb["Kt"], hb["SG"]

                    def state_update(blk):
                        for c in range(2):
                            bk = banks[4 + c]
                            mm(bk[:], Kh[blk][:, c * 128:(c + 1) * 128], Vv[blk][:], True, True,
                               [Kh[blk], Vv[blk]], [bk])
                            s_ = state[h * 2 + c]
                            stt(s_[:], s_[:], gamC[h], bk[:], ALU.mult, ALU.add, [s_, bk], [s_])

                    for i in range(4):
                        state_update(2 * i)
                        blk = 2 * i + 1
                        for c in range(2):
                            cp_alt(state_bf[c][:], state[2 * h + c][:], [state[2 * h + c]], [state_bf[c]])
                        qT_, kT_ = QT[i % 2], KT[i % 2]
                        transpose_to(qT_[:], [Qt[i][:, c * 128:(c + 1) * 128] for c in range(2)], 128, 2, [Qt[i]], [qT_])
                        transpose_to(kT_[:], [Kt[i][:, c * 128:(c + 1) * 128] for c in range(2)], 128, 2, [Kt[i]], [kT_])
                        yield 2
                        bs = banks[4]
                        for c in range(2):
                            mm(bs[:, 0:128], kT_[:, c, :], qT_[:, c, :], c == 0, c == 1, [kT_, qT_], [bs])
                        sc = scb[i % 2]
                        tt(sc[:], bs[:, 0:128], tri[:], ALU.mult, [bs, tri], [sc])
                        bo = banks[5]
                        mm(bo[:], sc[:], Vv[blk][:], True, False, [sc, Vv[blk]], [bo])
                        for c in range(2):
                            mm(bo[:], qT_[:, c, :], state_bf[c][:], False, c == 1, [qT_, state_bf[c]], [bo])
                        ot = ost[i % 2]
                        act(junk[:, 0:512], bo[:], AF.Square, [bo], [junk, ot], scale=1.0 / math.sqrt(512.0),
                            accum_out=ot[:, 0:1])
                        rstd_from_ms(ot[:, 0:1], ot[:, 0:1], [ot], [ot])
                        rg_ = rog[i % 2]
                        stt(rg_[:], bo[:], ot[:, 0:1], SG[i][:], ALU.mult, ALU.mult, [bo, ot, SG[i]], [rg_])
                        yield 2
                        transpose_to(roT[:, 4 * h:4 * h + 4, i * 128:(i + 1) * 128],
                                     [rg_[:, c * 128:(c + 1) * 128] for c in range(4)], 128, 4, [rg_], [roT_d[i]])
                        yield 1
                        state_update(blk)
                        yield 1

                for _ in proj_gen(0):
                    pass
                for h in range(4):
                    pg_it = proj_gen(h + 1) if h < 3 else None
                    for n_ in ret_gen(h):
                        for _ in range(n_):
                            if pg_it is not None:
                                try:
                                    next(pg_it)
                                except StopIteration:
                                    pg_it = None
                    if pg_it is not None:
                        for _ in pg_it:
                            pass
                A.pop()
                A.pop()
                S.barrier(NOPOOL)

                A.push()
                mixed = [A.tile([128, D], F32) for _ in range(4)]
                mixb = [A.tile([128, D], BF16) for _ in range(2)]
                mixT = A.view([128, 8, 512], BF16)
                mixT_d = [Dep() for _ in range(4)]
                sig = [A.tile([128, 512], F32) for _ in range(2)]
                h2T = A.view([128, 8, 512], BF16)
                h2T_d = [Dep() for _ in range(4)]
                actT = A.view([128, 16, 512], BF16)
                actT_d = [Dep() for _ in range(16)]
                pT = [A.tile([128, 2, 128], BF16) for _ in range(4)]
                rl = [A.tile([128, 512], F32) for _ in range(2)]

                def lhs_ro(b, k):
                    return roT[:, k, b * 128:(b + 1) * 128], [roT_d[b]]

                def lhs_ot(b, k):
                    return OT[:, k, (4 * g + b) * 128:(4 * g + b + 1) * 128], [OT_d[g]]

                def lhs_xo(b, k):
                    return xnTo[b][:, k, :], [xnTo[b]]

                for (wbr, Kbr, lhs_br, goff, first) in ((w_br, 2048, lhs_ro, OFF_GR, True),
                                                        (w_bm, 1024, lhs_ot, OFF_GM, False)):
                    for ct in range(2):
                        held = {}

                        def hold(b, bk, held=held):
                            held[b] = bk
                        proj_tm(wbr, Kbr, ct * 512, 512, lhs_br, [0, 1, 2, 3], hold, lambda bi: bi)

                        def gate_evac(b, bk, ct=ct, first=first, held=held):
                            sg = sig[b % 2]
                            sigmoid_from(bk, 512, sg)
                            mx = mixed[b]
                            if first:
                                tt(mx[:, ct * 512:(ct + 1) * 512], sg[:], held[b][:], ALU.mult, [sg, held[b]], [mx])
                            else:
                                tt(sg[:], sg[:], held[b][:], ALU.mult, [sg, held[b]], [sg])
                                tt(mx[:, ct * 512:(ct + 1) * 512], mx[:, ct * 512:(ct + 1) * 512], sg[:], ALU.add,
                                   [mx, sg], [mx])
                        proj_tm(w_in, 1024, goff + ct * 512, 512, lhs_xo, [0, 1, 2, 3], gate_evac,
                                lambda bi: 4 + bi % 2)
                for b in range(4):
                    mb = mixb[b % 2]
                    cp("act", mb[:], mixed[b][:], [mixed[b]], [mb])
                    transpose_to(mixT[:, :, b * 128:(b + 1) * 128], [mb[:, c * 128:(c + 1) * 128] for c in range(8)],
                                 128, 8, [mb], [mixT_d[b]])

                def lhs_mix(b, k):
                    return mixT[:, k, b * 128:(b + 1) * 128], [mixT_d[b]]

                def res_evac(ct):
                    def ev(b, bk):
                        tt(xown[b][:, ct * 512:(ct + 1) * 512], xown[b][:, ct * 512:(ct + 1) * 512], bk[:], ALU.add,
                           [xown[b], bk], [xown[b]])
                    return ev

                for ct in range(2):
                    proj_tm(w_o, 1024, ct * 512, 512, lhs_mix, [0, 1, 2, 3], res_evac(ct), lambda bi: bi % 6)
                bload(Gbuf, g_mlp)
                for b in range(4):
                    norm_to_T(b, Gbuf, h2T, h2T_d)

                def lhs_act(b, k):
                    return actT[:, k, b * 128:(b + 1) * 128], [actT_d[k]]

                for hf in range(2):
                    for ft in range(4):
                        sl = wslot(w_up, 0, 8, hf * 2048 + ft * 512, 512)
                        for fi in range(4):
                            f = ft * 4 + fi
                            bk = banks[4 + f % 2]
                            for c in range(8):
                                mm(bk[:], sl[:, c, fi * 128:(fi + 1) * 128], h2T[:, c, :], c == 0, c == 7,
                                   [sl] + h2T_d, [bk])
                            r_ = rl[f % 2]
                            act(r_[:], bk[:], AF.Relu, [bk], [r_])
                            act(actT[:, f, :], r_[:], AF.Square, [r_], [actT_d[f]])
                    for ct in range(2):
                        proj_tm(w_down[hf * 2048:(hf + 1) * 2048, :], 2048, ct * 512, 512, lhs_act, [0, 1, 2, 3],
                                res_evac(ct), lambda bi: bi)

                bload(Gbuf, g_ple)
                for b in range(4):
                    norm_to_T(b, Gbuf, h2T, h2T_d)
                    pb_ = pbf[b % 2]
                    S.dma("pool", pb_[:], pp[4 * g + b], [], [pb_])
                    transpose_to(pT[b][:], [pb_[:, c * 128:(c + 1) * 128] for c in range(2)], 128, 2, [pb_], [pT[b]])

                def lhs_h3(b, k):
                    return h2T[:, k, b * 128:(b + 1) * 128], [h2T_d[b]]

                for ct in range(2):
                    def ple_evac(b, bk, ct=ct):
                        sg = sig[b % 2]
                        sigmoid_from(bk, 512, sg)
                        bp = banks[4 + b % 2]
                        for c in range(2):
                            mm(bp[:], pT[b][:, c, :], Wple[:, c, ct * 512:(ct + 1) * 512], c == 0, c == 1,
                               [pT[b], Wple], [bp])
                        tt(sg[:], sg[:], bp[:], ALU.mult, [sg, bp], [sg])
                        tt(xown[b][:, ct * 512:(ct + 1) * 512], xown[b][:, ct * 512:(ct + 1) * 512], sg[:], ALU.add,
                           [xown[b], sg], [xown[b]])
                    proj_tm(w_pg, 1024, ct * 512, 512, lhs_h3, [0, 1, 2, 3], ple_evac, lambda bi: bi)
                for b in range(4):
                    S.dma("sp", out_d[4 * g + b], xown[b][:], [xown[b]], [])
                A.pop()
            A.pop()

        S.barrier()
        S.emit()
    return nc, dbg


def make_in_maps(x, p, positions, norm_mix, w_in, ret_norm, q_lat_norm, kv_lat_norm, w_uq, w_ukv,
                 q_norm, k_norm, w_br, w_bm, w_o, norm_mlp, w_up, w_down, norm_ple, w_ple_gate, w_ple):
    f32 = np.float32
    x = np.asarray(x, f32)
    p = np.asarray(p, f32)
    positions = np.asarray(positions, np.int32)
    shared = {
        "w_in": np.ascontiguousarray(np.asarray(w_in, f32)[0]),
        "w_uq": np.ascontiguousarray(np.asarray(w_uq, f32)[0]),
        "w_ukv": np.ascontiguousarray(np.asarray(w_ukv, f32)[0]),
        "w_br": np.ascontiguousarray(np.asarray(w_br, f32)[0]),
        "w_bm": np.ascontiguousarray(np.asarray(w_bm, f32)[0]),
        "w_o": np.ascontiguousarray(np.asarray(w_o, f32)[0]),
        "w_up": np.ascontiguousarray(np.asarray(w_up, f32)[0]),
        "w_down": np.ascontiguousarray(np.asarray(w_down, f32)[0]),
        "w_pg": np.ascontiguousarray(np.asarray(w_ple_gate, f32)[0]),
        "w_ple": np.ascontiguousarray(np.asarray(w_ple, f32)[0]),
        "g_mix": np.asarray(norm_mix, f32).reshape(1, -1),
        "g_ret": np.asarray(ret_norm, f32).reshape(1, -1),
        "g_ql": np.asarray(q_lat_norm, f32).reshape(1, -1),
        "g_kl": np.asarray(kv_lat_norm, f32).reshape(1, -1),
        "g_q": np.asarray(q_norm, f32).reshape(1, -1),
        "g_k": np.asarray(k_norm, f32).reshape(1, -1),
        "g_mlp": np.asarray(norm_mlp, f32).reshape(1, -1),
        "g_ple": np.asarray(norm_ple, f32).reshape(1, -1),
    }
    shared["c_ident"] = np.eye(128, dtype=f32)
    kk = np.arange(128)
    shared["c_tri"] = (kk[:, None] <= kk[None, :]).astype(f32)
    shared["c_invm"] = (f32(10000.0) ** (-np.arange(32, dtype=f32) / f32(32))).astype(f32).reshape(1, 32)
    shared["c_invr"] = (f32(10000.0) ** (-np.arange(128, dtype=f32) / f32(128))).astype(f32).reshape(1, 128)
    t = np.arange(128, dtype=np.float64)
    dec = np.zeros((128, 12), np.float64)
    for h in range(4):
        gmm = 1.0 - 2.0 ** (-5 - h)
        dec[:, h] = gmm ** (t + 1.0)
        dec[:, 4 + h] = gmm ** (-(t + 1.0)) / 16.0
        dec[:, 8 + h] = gmm ** (127.0 - t) / 16.0
    shared["c_dec"] = dec.astype(f32)
    maps = []
    for c in range(8):
        b, j = c // 2, c % 2
        xb = x[b].reshape(NB, 128, D)
        pb = positions[b].reshape(NB, 128)
        xo_ = np.ascontiguousarray(xb[j::2])
        po = pb[j::2]
        kb = np.zeros((128, NB), f32)
        if j == 1:
            xt_ = np.ascontiguousarray(xb[0::2])
            pt = pb[0::2]
        else:
            xt_ = np.concatenate([np.zeros((1, 128, D), f32), xb[1:NB - 1:2]], axis=0)
            pt = np.concatenate([np.zeros((1, 128), np.int32), pb[1:NB - 1:2]], axis=0)
            kb[:, 0] = -30000.0
        post = np.zeros((128, NB), np.int32)
        post[:, 0::2] = pt.T
        post[:, 1::2] = po.T
        m = dict(shared)
        m["xo"] = xo_
        m["xt"] = np.ascontiguousarray(xt_)
        m["pp"] = np.ascontiguousarray(p[0, b].reshape(NB, 128, 256)[j::2])
        m["post"] = post
        m["kbias"] = kb
        maps.append(m)
    return maps


_CACHE = {}


def kernel(**inputs):
    if "nc" not in _CACHE:
        _CACHE["nc"] = build_program()[0]
    nc = _CACHE["nc"]
    maps = make_in_maps(**inputs)
    res = run_bass_kernel_spmd(nc, maps, core_ids=list(range(8)))
    out = np.zeros((4, NB, 128, D), np.float32)
    for c in range(8):
        b, j = c // 2, c % 2
        out[b, j::2] = np.asarray(res.results[c]["out"]).reshape(NOWN, 128, D)
    return out.reshape(4, NB * 128, D)
```

```python
import contextlib
import math

import numpy as np
import concourse.bass as bass
import concourse.mybir as mybir
from concourse.bass_utils import run_bass_kernel_spmd

F32 = mybir.dt.float32
BF16 = mybir.dt.bfloat16
I32 = mybir.dt.int32
AF = mybir.ActivationFunctionType
ALU = mybir.AluOpType
AX = mybir.AxisListType

D = 1024
NB = 32
NOWN = 16
EPS = 1e-6
OFF_RQ, OFF_RK, OFF_RV, OFF_RG, OFF_CQ, OFF_CKV, OFF_KR, OFF_GR, OFF_GM = (
    0, 1024, 2048, 4096, 6144, 6528, 6784, 6848, 7872)
NGRP = 4
TWO_PI = 2.0 * math.pi
C1 = 6.28125
C2 = TWO_PI - C1


class Dep:
    __slots__ = ("w", "r")

    def __init__(self):
        self.w = None
        self.r = []


class T:
    __slots__ = ("ap", "d")

    def __init__(self, ap, d=None):
        self.ap = ap
        self.d = d if d is not None else Dep()

    def __getitem__(self, k):
        return self.ap[k]


class Sched:
    ENGS = ("pe", "act", "dve", "pool", "sp")

    def __init__(self, nc, es, n_dma_sems=8):
        self.nc = nc
        self.ops = {e: [] for e in self.ENGS}
        self.cnt = {e: 0 for e in self.ENGS}
        self.waited = {e: {} for e in self.ENGS}
        self.semobj = {}
        for e in self.ENGS:
            self.semobj[("e", e)] = es.enter_context(nc.semaphore("sem_" + e))
        self.dq = {}
        for q in ("sp", "pool"):
            n = n_dma_sems
            for i in range(n):
                self.semobj[("d", q, i)] = es.enter_context(nc.semaphore("dq_%s_%d" % (q, i)))
            self.dq[q] = {"n": [0] * n, "next": 0}

    def _need(self, eng, ev, waits):
        if ev is None:
            return
        key, val = ev
        if eng == "pe" and key == ("e", "pe"):
            return
        if self.waited[eng].get(key, 0) >= val:
            return
        self.waited[eng][key] = val
        waits.append((key, val))

    def _deps(self, eng, reads, writes):
        waits = []
        for d in reads:
            self._need(eng, d.w, waits)
        for d in writes:
            self._need(eng, d.w, waits)
            for ev in d.r:
                self._need(eng, ev, waits)
        return waits

    @staticmethod
    def _commit(ev, reads, writes):
        for d in reads:
            d.r.append(ev)
        for d in writes:
            d.w = ev
            d.r = []

    def op(self, eng, fn, reads=(), writes=()):
        reads = [t.d if isinstance(t, T) else t for t in reads]
        writes = [t.d if isinstance(t, T) else t for t in writes]
        waits = self._deps(eng, reads, writes)
        self.cnt[eng] += 1
        ev = (("e", eng), self.cnt[eng])
        self.ops[eng].append((waits, fn, (("e", eng), 1)))
        self._commit(ev, reads, writes)

    def dma(self, q, out, in_, reads=(), writes=()):
        reads = [t.d if isinstance(t, T) else t for t in reads]
        writes = [t.d if isinstance(t, T) else t for t in writes]
        d = self.dq[q]
        i = d["next"]
        d["next"] = (i + 1) % len(d["n"])
        key = ("d", q, i)
        waits = self._deps(q, reads, writes)
        if d["n"][i]:
            self._need(q, (key, 16 * d["n"][i]), waits)
        d["n"][i] += 1
        ev = (key, 16 * d["n"][i])
        self.ops[q].append((waits, lambda e, o=out, s=in_: e.dma_start(out=o, in_=s), (key, 16)))
        self._commit(ev, reads, writes)

    def all_events(self):
        evs = [(("e", e), self.cnt[e]) for e in self.ENGS if self.cnt[e]]
        for q, d in self.dq.items():
            evs += [(("d", q, i), 16 * n) for i, n in enumerate(d["n"]) if n]
        return evs

    def barrier(self, engines=None):
        evs = self.all_events()
        for e in (engines or self.ENGS):
            waits = []
            for ev in evs:
                self._need(e, ev, waits)
            if waits:
                self.ops[e].append((waits, None, None))

    def emit(self):
        nc = self.nc
        so = self.semobj

        def run(e, name):
            for waits, fn, inc in self.ops[name]:
                for key, val in waits:
                    e.wait_ge(so[key], val)
                if fn is not None:
                    fn(e).then_inc(so[inc[0]], inc[1])

        with nc.Block() as block:
            @block.tensor
            def _(e):
                run(e, "pe")

            @block.scalar
            def _(e):
                run(e, "act")

            @block.vector
            def _(e):
                run(e, "dve")

            @block.gpsimd
            def _(e):
                run(e, "pool")

            @block.sync
            def _(e):
                run(e, "sp")


def build_program(stage=99, debug=False):
    nc = bass.Bass("TRN2", target_bir_lowering=False)

    def din(name, shape, dt=F32):
        return nc.dram_tensor(name, list(shape), dt, kind="ExternalInput").ap()

    xo = din("xo", [NOWN, 128, D])
    xt = din("xt", [NOWN, 128, D])
    pp = din("pp", [NOWN, 128, 256])
    post = din("post", [128, NB], I32)
    kbias_d = din("kbias", [128, NB])
    w_in = din("w_in", [D, 8896])
    w_uq = din("w_uq", [384, 1536])
    w_ukv = din("w_ukv", [256, 2048])
    w_br = din("w_br", [2048, D])
    w_bm = din("w_bm", [D, D])
    w_o = din("w_o", [D, D])
    w_up = din("w_up", [D, 4096])
    w_down = din("w_down", [4096, D])
    w_pg = din("w_pg", [D, D])
    w_ple = din("w_ple", [256, D])
    g_mix = din("g_mix", [1, D])
    g_ret = din("g_ret", [1, 2048])
    g_ql = din("g_ql", [1, 384])
    g_kl = din("g_kl", [1, 256])
    g_q = din("g_q", [1, 192])
    g_k = din("g_k", [1, 192])
    g_mlp = din("g_mlp", [1, D])
    g_ple = din("g_ple", [1, D])
    c_ident = din("c_ident", [128, 128])
    c_tri = din("c_tri", [128, 128])
    c_invm = din("c_invm", [1, 32])
    c_invr = din("c_invr", [1, 128])
    c_dec = din("c_dec", [128, 12])
    out_d = nc.dram_tensor("out", [NOWN, 128, D], F32, kind="ExternalOutput").ap()
    dbg = {}

    def dbg_out(name, shape, dt=F32):
        dbg[name] = nc.dram_tensor(name, list(shape), dt, kind="ExternalOutput").ap()
        return dbg[name]

    with contextlib.ExitStack() as es:
        S = Sched(nc, es)
        ARENA_COLS = 106300
        arena = es.enter_context(nc.sbuf_tensor("arena", [128, ARENA_COLS], BF16))
        banks = [T(es.enter_context(nc.psum_tensor("bank%d" % i, [128, 512], F32))[:]) for i in range(8)]

        class Alloc:
            def __init__(self):
                self.top = 0
                self.marks = []

            def view(self, shape, dt, parts=128):
                n = 1
                for s in shape[1:]:
                    n *= s
                cols = n if dt == BF16 else 2 * n
                cols = (cols + 15) // 16 * 16
                assert self.top + cols <= ARENA_COLS, ("SBUF arena overflow", self.top, cols, shape)
                ap = arena[0:shape[0], self.top:self.top + (n if dt == BF16 else 2 * n)]
                self.top += cols
                if dt != BF16:
                    ap = ap.bitcast(dt)
                if len(shape) == 3:
                    ap = ap.rearrange("p (a b) -> p a b", a=shape[1])
                elif len(shape) == 4:
                    ap = ap.rearrange("p (a b c) -> p a b c", a=shape[1], b=shape[2])
                return ap

            def tile(self, shape, dt):
                return T(self.view(shape, dt))

            def push(self):
                self.marks.append(self.top)

            def pop(self):
                self.top = self.marks.pop()

        A = Alloc()

        def act(out, in_, func, r, w, **kw):
            S.op("act", lambda e: e.activation(out=out, in_=in_, func=func, **kw), r, w)

        def tt(out, in0, in1, op, r, w, eng="dve"):
            S.op(eng, lambda e: e.tensor_tensor(out=out, in0=in0, in1=in1, op=op), r, w)

        def ts(out, in0, s1, s2, op0, op1, r, w, eng="dve"):
            if s2 is None:
                S.op(eng, lambda e: e.tensor_scalar(out=out, in0=in0, scalar1=s1, scalar2=None, op0=op0), r, w)
            else:
                S.op(eng, lambda e: e.tensor_scalar(out=out, in0=in0, scalar1=s1, scalar2=s2, op0=op0, op1=op1), r, w)

        def stt(out, in0, scalar, in1, op0, op1, r, w, eng="dve"):
            S.op(eng, lambda e: e.scalar_tensor_tensor(out=out, in0=in0, scalar=scalar, in1=in1, op0=op0, op1=op1), r, w)

        def cp(eng, out, in_, r, w):
            if eng == "act":
                S.op("act", lambda e: e.copy(out=out, in_=in_), r, w)
            else:
                S.op(eng, lambda e: e.tensor_copy(out=out, in_=in_), r, w)

        def mm(out, lhsT, rhs, start, stop, r, w):
            S.op("pe", lambda e: e.matmul(out, lhsT=lhsT, rhs=rhs, start=start, stop=stop), r, w)

        def memset(ap, val, w, eng="dve"):
            S.op(eng, lambda e: e.memset(ap, val), [], w)

        cpi = [0]
        cp_pref = ["alt"]

        def cp_alt(out, in_, r, w):
            cpi[0] += 1
            if cp_pref[0] == "alt":
                cp("act" if cpi[0] % 2 else "dve", out, in_, r, w)
            else:
                cp(cp_pref[0], out, in_, r, w)

        ident = A.tile([128, 128], BF16)
        tri = A.tile([128, 128], BF16)
        ones = A.tile([128, 128], BF16)
        epsc = A.tile([128, 1], F32)
        post_i = A.tile([128, NB], I32)
        post_f = A.tile([128, NB], F32)
        kbias = A.tile([128, NB], F32)
        dec = A.tile([128, 12], F32)
        S.dma("pool", ident[:], c_ident, [], [ident])
        S.dma("pool", tri[:], c_tri, [], [tri])
        S.dma("sp", post_i[:], post, [], [post_i])
        S.dma("sp", kbias[:], kbias_d, [], [kbias])
        S.dma("sp", dec[:], c_dec, [], [dec])
        memset(ones[:], 1.0, [ones])
        memset(epsc[:], EPS, [epsc])
        onec = A.tile([128, 1], F32)
        memset(onec[:], 1.0, [onec])
        cp("dve", post_f[:], post_i[:], [post_i], [post_f])

        trn = [0]
        tbanks = [6, 7]

        def transpose_to(dst_ap, src_ap, ncols_in, nblk, r_src, w_dst, in_parts=128, evac="alt"):
            bk = banks[tbanks[trn[0] % len(tbanks)]]
            trn[0] += 1
            pv = bk.ap.bitcast(BF16).rearrange("p (a b) -> p a b", a=8)
            for i in range(nblk):
                S.op("pe", lambda e, i=i: e.transpose(out=pv[0:ncols_in, i, :], in_=src_ap[i], identity=ident[:]),
                     list(r_src) + [ident], [bk])
            if evac == "alt":
                cp_alt(dst_ap, pv[0:ncols_in, 0:nblk, :], [bk], w_dst)
            else:
                cp(evac, dst_ap, pv[0:ncols_in, 0:nblk, :], [bk], w_dst)

        def rstd_from_ms(rstd_ap, ms_ap, r, w):
            act(rstd_ap, ms_ap, AF.Ln, r + [epsc], w, bias=epsc[:, 0:1], scale=1.0)
            act(rstd_ap, rstd_ap, AF.Exp, w, w, scale=-0.5)

        def sincos(cs, kf, ki, r, n, do_sin=True):
            ts(cs[:, n:2 * n], cs[:, n:2 * n], 0.5 * math.pi, None, ALU.add, None, [cs], [cs])
            ts(kf[:], cs[:], 1.0 / TWO_PI, None, ALU.mult, None, [cs], [kf])
            cp("dve", ki[:], kf[:], [kf], [ki])
            cp("dve", kf[:], ki[:], [ki], [kf])
            stt(cs[:], kf[:], -C1, cs[:], ALU.mult, ALU.add, [kf, cs], [cs])
            stt(cs[:], kf[:], -C2, cs[:], ALU.mult, ALU.add, [kf, cs], [cs])
            ts(kf[:], cs[:], math.pi, -TWO_PI, ALU.is_gt, ALU.mult, [cs], [kf])
            tt(cs[:], cs[:], kf[:], ALU.add, [cs, kf], [cs])
            if do_sin:
                act(cs[:], cs[:], AF.Sin, [cs], [cs])

        def bload(dst, src_row, eng="sp"):
            S.dma(eng, dst[:], src_row.partition_broadcast(128), [], [dst])

        NSLOT = 3
        OT = A.view([128, 8, NOWN * 128], BF16)
        OT_d = [Dep() for _ in range(NGRP)]
        A.push()

        ckvT = A.view([128, 2, NB * 128], BF16)
        ckvT_d = [Dep() for _ in range(NB)]
        KTr = A.view([128, NB * 128], BF16)
        KTr_z = Dep()
        S.op("dve", lambda e: e.memset(KTr[64:128, :], 0.0), [], [KTr_z])
        KTr_d = [Dep() for _ in range(NB)]
        cqT = A.view([128, 3, NOWN * 128], BF16)
        cqT_d = [Dep() for _ in range(NOWN)]
        ssr = A.tile([128, NB], F32)
        csM = A.tile([128, 2 * NB * 32], F32)
        sinM = csM.ap[:, 0:NB * 32].rearrange("p (a b) -> p a b", a=NB)
        cosM = csM.ap[:, NB * 32:2 * NB * 32].rearrange("p (a b) -> p a b", a=NB)
        G_k = A.tile([128, 192], F32)
        G_q = A.tile([128, 192], F32)
        gk_col = A.tile([128, 1], F32)
        bload(G_k, g_k)
        bload(G_q, g_q)
        S.dma("sp", gk_col[:], g_k[0:1, 0:128].rearrange("o d -> d o"), [], [gk_col])

        A.push()
        G_mix = A.tile([128, D], F32)
        G_ql = A.tile([128, 384], F32)
        G_kl = A.tile([128, 256], F32)
        invM = A.tile([128, 32], F32)
        bload(G_mix, g_mix)
        bload(G_ql, g_ql)
        bload(G_kl, g_kl)
        bload(invM, c_invm)
        WA = A.tile([128, 8, 320], BF16)
        WB = A.tile([128, 8, 384], BF16)
        w_in_v = w_in.rearrange("(c p) n -> p c n", p=128)
        S.dma("pool", WA[:], w_in_v[:, :, OFF_CKV:OFF_CKV + 320], [], [WA])
        S.dma("pool", WB[:], w_in_v[:, :, OFF_CQ:OFF_CQ + 384], [], [WB])
        kfM = A.tile([128, 2 * NB * 32], F32)
        kiM = A.tile([128, 2 * NB * 32], I32)
        for half in range(2):
            o3 = csM.ap[:, half * NB * 32:(half + 1) * NB * 32].rearrange("p (a b) -> p a b", a=NB)
            tt(o3, post_f[:].unsqueeze(2).to_broadcast([128, NB, 32]),
               invM[:].unsqueeze(1).to_broadcast([128, NB, 32]), ALU.mult, [post_f, invM], [csM])
        sincos(csM, kfM, kiM, [], NB * 32)

        junk = A.tile([128, D], BF16)

        def norm_batch_gen(items, G, st, c0):
            n = len(items)
            for k, (xb, xn_, xT, src) in enumerate(items):
                if src is not None:
                    S.dma("sp", xb[:], src, [], [xb])
                act(junk[:], xb[:], AF.Square, [xb], [junk, st], scale=1.0 / 32.0, accum_out=st[:, c0 + k:c0 + k + 1])
                yield
            rstd_from_ms(st[:, c0:c0 + n], st[:, c0:c0 + n], [st], [st])
            yield
            for k, (xb, xn_, xT, src) in enumerate(items):
                stt(xn_[:], xb[:], st[:, c0 + k:c0 + k + 1], G[:], ALU.mult, ALU.mult, [xb, st, G], [xn_])
                yield
            for k, (xb, xn_, xT, src) in enumerate(items):
                transpose_to(xT[:], [xn_[:, c * 128:(c + 1) * 128] for c in range(8)], 128, 8, [xn_], [xT])
                yield

        def norm_batch(items, G, st, c0):
            for _ in norm_batch_gen(items, G, st, c0):
                pass

        NBT = 4
        xb1 = [[A.tile([128, D], F32) for _ in range(NBT)] for _ in range(2)]
        xn1 = [[A.tile([128, D], BF16) for _ in range(NBT)] for _ in range(2)]
        xT1 = [[A.tile([128, 8, 128], BF16) for _ in range(NBT)] for _ in range(2)]
        st1 = [A.tile([128, 16], F32) for _ in range(2)]
        for t_ in st1:
            memset(t_[:], 1.0, [t_])
        ckvn = [A.tile([128, 256], BF16) for _ in range(NBT)]
        cqn = [A.tile([128, 384], BF16) for _ in range(NBT)]
        krg = [A.tile([128, 64], F32) for _ in range(NBT)]
        krA = [A.tile([128, 64], F32) for _ in range(NBT)]
        krB = [A.tile([128, 64], F32) for _ in range(NBT)]
        krr = [A.tile([128, 64], BF16) for _ in range(NBT)]

        nb1 = NB if stage >= 1 else 0

        def p1_norm(p0):
                par = (p0 // NBT) % 2
                st = st1[par]
                ps = list(range(p0, p0 + NBT))
                yield from norm_batch_gen([(xb1[par][k], xn1[par][k], xT1[par][k], (xo if p % 2 else xt)[p // 2])
                                       for k, p in enumerate(ps)], G_mix, st, 0)

        def p1_latent(p0):
                par = (p0 // NBT) % 2
                st = st1[par]
                ps = list(range(p0, p0 + NBT))
                bkB_of = {}
                for k, p in enumerate(ps):
                    xT = xT1[par][k]
                    bkA = banks[k]
                    for c in range(8):
                        mm(bkA[:, 0:320], xT[:, c, :], WA[:, c, :], c == 0, c == 7, [xT, WA], [bkA])
                    if p % 2 == 1:
                        bkB = banks[4 + (k // 2) % 2]
                        bkB_of[k] = bkB
                        for c in range(8):
                            mm(bkB[:, 0:384], xT[:, c, :], WB[:, c, :], c == 0, c == 7, [xT, WB], [bkB])
                    yield
                for k, p in enumerate(ps):
                    bkA = banks[k]
                    act(junk[:, 0:256], bkA[:, 0:256], AF.Square, [bkA], [junk, st], scale=1.0 / 16.0,
                        accum_out=st[:, 4 + k:5 + k])
                    act(junk[:, 256:320], bkA[:, 256:320], AF.Square, [bkA], [junk, ssr], accum_out=ssr[:, p:p + 1])
                    if p % 2 == 1:
                        act(junk[:, 0:384], bkB_of[k][:, 0:384], AF.Square, [bkB_of[k]], [junk, st],
                            scale=1.0 / math.sqrt(384.0), accum_out=st[:, 8 + k:9 + k])
                    yield
                rstd_from_ms(st[:, 4:12], st[:, 4:12], [st], [st])
                yield
                for k, p in enumerate(ps):
                    bkA = banks[k]
                    cn = ckvn[k]
                    stt(cn[:], bkA[:, 0:256], st[:, 4 + k:5 + k], G_kl[:], ALU.mult, ALU.mult, [bkA, st, G_kl], [cn])
                    kg, kA, kB, kr = krg[k], krA[k], krB[k], krr[k]
                    tt(kg[:], bkA[:, 256:320], G_k[:, 128:192], ALU.mult, [bkA, G_k], [kg])
                    kg3 = kg[:].rearrange("p (a b) -> p a b", a=2)
                    tt(kA[:].rearrange("p (a b) -> p a b", a=2), kg3,
                       cosM[:, p, :].unsqueeze(1).to_broadcast([128, 2, 32]), ALU.mult, [kg, csM], [kA])
                    tt(kB[:].rearrange("p (a b) -> p a b", a=2), kg3,
                       sinM[:, p, :].unsqueeze(1).to_broadcast([128, 2, 32]), ALU.mult, [kg, csM], [kB])
                    tt(kr[:, 0:32], kA[:, 0:32], kB[:, 32:64], ALU.subtract, [kA, kB], [kr])
                    tt(kr[:, 32:64], kA[:, 32:64], kB[:, 0:32], ALU.add, [kA, kB], [kr])
                    if p % 2 == 1:
                        qn_ = cqn[k]
                        stt(qn_[:], bkB_of[k][:, 0:384], st[:, 8 + k:9 + k], G_ql[:], ALU.mult, ALU.mult,
                            [bkB_of[k], st, G_ql], [qn_])
                    yield
                for k, p in enumerate(ps):
                    m = p // 2
                    cn, kr = ckvn[k], krr[k]
                    transpose_to(ckvT[:, :, p * 128:(p + 1) * 128], [cn[:, c * 128:(c + 1) * 128] for c in range(2)],
                                 128, 2, [cn], [ckvT_d[p]])
                    transpose_to(KTr[0:64, p * 128:(p + 1) * 128].unsqueeze(1), [kr[:, 0:64]], 64, 1, [kr], [KTr_d[p]])
                    if p % 2 == 1:
                        qn_ = cqn[k]
                        transpose_to(cqT[:, :, m * 128:(m + 1) * 128], [qn_[:, c * 128:(c + 1) * 128] for c in range(3)],
                                     128, 3, [qn_], [cqT_d[m]])
                    yield

        def zip_gens(ga, gb):
            da = db = False
            while not (da and db):
                if not da:
                    try:
                        next(ga)
                    except StopIteration:
                        da = True
                if not db:
                    try:
                        next(gb)
                    except StopIteration:
                        db = True

        p0s = list(range(0, nb1, NBT))
        if p0s:
            for _ in p1_norm(p0s[0]):
                pass
        for n_, p0 in enumerate(p0s):
            nxt = p1_norm(p0s[n_ + 1]) if n_ + 1 < len(p0s) else iter(())
            zip_gens(p1_latent(p0), nxt)
        if debug and stage >= 1:
            S.barrier()
            d1 = dbg_out("d_ckvT", [128, 2, NB * 128], BF16)
            S.dma("sp", d1, ckvT, ckvT_d, [])
            d2 = dbg_out("d_KTr", [64, NB * 128], BF16)
            S.dma("sp", d2, KTr[0:64, :], KTr_d, [])
            d3 = dbg_out("d_cqT", [128, 3, NOWN * 128], BF16)
            S.dma("sp", d3, cqT, cqT_d, [])
            d4 = dbg_out("d_ssr", [128, NB])
            S.dma("sp", d4, ssr[:], [ssr], [])
        A.pop()
        S.barrier()

        A.push()
        if stage >= 2:
            Wkv = A.tile([128, 2, 2048], BF16)
            Wq = A.tile([128, 3, 1536], BF16)
            S.dma("pool", Wkv[:], w_ukv.rearrange("(c p) n -> p c n", p=128), [], [Wkv])
            S.dma("pool", Wq[:], w_uq.rearrange("(c p) n -> p c n", p=128), [], [Wq])
            Wkv4 = Wkv[:].rearrange("p c (h t d) -> p c h t d", t=2, d=128)
            KTn = A.view([128, 4, NB * 128], BF16)
            KTn_d = [[Dep() for _ in range(8)] for _ in range(4)]
            Vt = A.view([128, NB, 4, 128], BF16)
            Vt_d = [Dep() for _ in range(NB)]
            ssk = A.tile([128, NB, 4], F32)
            ksc = A.tile([128, NB, 4], F32)
            sq = [A.tile([128, 512], F32) for _ in range(2)]
            QTn = [A.view([128, 4, 512], BF16) for _ in range(2)]
            QTn_d = [[Dep() for _ in range(4)] for _ in range(2)]
            QTr = [A.view([128, 4, 512], BF16) for _ in range(2)]
            QTr_z = Dep()
            for q_ in QTr:
                S.op("dve", lambda e, q_=q_: e.memset(q_[64:128, :, :], 0.0), [], [QTr_z])
            QTr_d = [[Dep() for _ in range(4)] for _ in range(2)]
            qst = [A.tile([128, 8], F32) for _ in range(2)]
            qn32 = [A.tile([128, 4, 192], F32) for _ in range(2)]
            qA = [A.tile([128, 4, 64], F32) for _ in range(2)]
            qB = [A.tile([128, 4, 64], F32) for _ in range(2)]
            qbf = [A.tile([128, 4, 192], BF16) for _ in range(2)]
            NPB = 4
            Pb = [A.tile([128, 512], BF16) for _ in range(NPB)]
            rec = A.tile([128, 512], F32)
            lnsc = A.tile([128, 1], F32)
            memset(lnsc[:], -0.5 * math.log(192.0), [lnsc])
            pbi = 0
            sbi = 0
            tbanks[:] = [7]
            SBK = [0, 1, 6]
            seq = [(h_, g_) for h_ in range(2) for g_ in range(NGRP)]

            def qbuild(hh, g, i, qb):
                m = 4 * g + i
                p_own = 2 * m + 1
                st = qst[i % 2]
                q32, qa, qb_, qh = qn32[i % 2], qA[i % 2], qB[i % 2], qbf[i % 2]
                for half in range(2):
                    bk = banks[(2 * i + half) % 2]
                    c0 = (4 * hh + 2 * half) * 192
                    for c in range(3):
                        mm(bk[:, 0:384], cqT[:, c, m * 128:(m + 1) * 128], Wq[:, c, c0:c0 + 384],
                           c == 0, c == 2, [cqT_d[m], Wq], [bk])
                    s_ = sq[half]
                    act(s_[:, 0:384], bk[:, 0:384], AF.Square, [bk], [s_], scale=1.0 / math.sqrt(192.0))
                    S.op("dve", lambda e, s_=s_, st=st, half=half: e.tensor_reduce(
                        out=st[:, 2 * half:2 * half + 2], in_=s_[:, 0:384].rearrange("p (h d) -> p h d", h=2),
                        axis=AX.X, op=ALU.add), [s_], [st])
                    rstd_from_ms(st[:, 2 * half:2 * half + 2], st[:, 2 * half:2 * half + 2], [st], [st])
                    for hq in range(2):
                        stt(q32[:, 2 * half + hq, :], bk[:, hq * 192:(hq + 1) * 192],
                            st[:, 2 * half + hq:2 * half + hq + 1], G_q[:], ALU.mult, ALU.mult,
                            [bk, st, G_q], [q32])
                qr4 = q32[:, :, 128:192].rearrange("p h (a b) -> p h a b", a=2)
                tt(qa[:].rearrange("p h (a b) -> p h a b", a=2), qr4,
                   cosM[:, p_own, :].unsqueeze(1).unsqueeze(1).to_broadcast([128, 4, 2, 32]), ALU.mult,
                   [q32, csM], [qa])
                tt(qb_[:].rearrange("p h (a b) -> p h a b", a=2), qr4,
                   sinM[:, p_own, :].unsqueeze(1).unsqueeze(1).to_broadcast([128, 4, 2, 32]), ALU.mult,
                   [q32, csM], [qb_])
                tt(qh[:, :, 128:160], qa[:, :, 0:32], qb_[:, :, 32:64], ALU.subtract, [qa, qb_], [qh])
                tt(qh[:, :, 160:192], qa[:, :, 32:64], qb_[:, :, 0:32], ALU.add, [qa, qb_], [qh])
                cp("act", qh[:, :, 0:128], q32[:, :, 0:128], [q32], [qh])
                transpose_to(QTn[qb][:, :, i * 128:(i + 1) * 128], [qh[:, hl, 0:128] for hl in range(4)],
                             128, 4, [qh], [QTn_d[qb][i]])
                transpose_to(QTr[qb][0:64, :, i * 128:(i + 1) * 128], [qh[:, hl, 128:192] for hl in range(4)],
                             64, 4, [qh], [QTr_d[qb][i]])


            def attend(hh, g, hl, qb):
                nonlocal pbi, sbi
                h = 4 * hh + hl
                bO = banks[2 + hl % 2]
                bL = banks[4 + hl % 2]
                plist = list(range(8 * g + 8))
                pend = []

                def finish(item, first, last):
                    p_, i0_, Pt_ = item
                    n0 = i0_ * 128
                    mm(bO[:, n0:512], Vt[:, p_, hl, :], Pt_[:, n0:512], first, last, [Vt_d[p_], Pt_], [bO])
                    mm(bL[:, n0:512], ones[:], Pt_[:, n0:512], first, last, [ones, Pt_], [bL])

                for idx, p in enumerate(plist):
                    r = p - 8 * g
                    i0 = 0 if r <= 1 else r // 2
                    n0 = i0 * 128
                    bS = banks[SBK[sbi % 3]]
                    sbi += 1
                    mm(bS[:, n0:512], KTn[:, hl, p * 128:(p + 1) * 128], QTn[qb][:, hl, n0:512], True, False,
                       [KTn_d[hl][p // 4]] + QTn_d[qb][i0:4], [bS])
                    mm(bS[:, n0:512], KTr[:, p * 128:(p + 1) * 128], QTr[qb][:, hl, n0:512], False, True,
                       [KTr_d[p], KTr_z, QTr_z] + QTr_d[qb][i0:4], [bS])
                    Pt = Pb[pbi % NPB]
                    pbi += 1
                    act(Pt[:, n0:512], bS[:, n0:512], AF.Exp, [bS, ksc, kbias], [Pt],
                        bias=kbias[:, p:p + 1], scale=ksc[:, p, hl:hl + 1])
                    if r >= 1 and r % 2 == 1:
                        tt(Pt[:, n0:n0 + 128], Pt[:, n0:n0 + 128], tri[:], ALU.mult, [Pt, tri], [Pt])
                    pend.append((p, i0, Pt))
                    if len(pend) > 2:
                        it_ = pend.pop(0)
                        finish(it_, it_[0] == 0, False)
                while pend:
                    it_ = pend.pop(0)
                    finish(it_, it_[0] == 0, len(pend) == 0)
                S.op("dve", lambda e, bL=bL: e.reciprocal(out=rec[:], in_=bL[:]), [bL], [rec])
                tt(OT[:, h, g * 512:(g + 1) * 512], bO[:], rec[:], ALU.mult, [bO, rec], [OT_d[g]])

            for hh in range(2):
                def kvA():
                    for p in range(NB):
                        bkv = banks[(2 * p) % 4]
                        bkk = banks[(2 * p + 1) % 4]
                        for c in range(2):
                            mm(bkv[:].rearrange("p (h d) -> p h d", h=4), ckvT[:, c, p * 128:(p + 1) * 128],
                               Wkv4[:, c, 4 * hh:4 * hh + 4, 1, :], c == 0, c == 1, [ckvT_d[p], Wkv], [bkv])
                        cp_alt(Vt[:, p, :, :], bkv[:].rearrange("p (h d) -> p h d", h=4), [bkv], [Vt_d[p]])
                        for c in range(2):
                            mm(bkk[:].rearrange("p (h d) -> p h d", h=4), ckvT[:, c, p * 128:(p + 1) * 128],
                               Wkv4[:, c, 4 * hh:4 * hh + 4, 0, :], c == 0, c == 1, [ckvT_d[p], Wkv], [bkk])
                        s_ = sq[p % 2]
                        act(s_[:], bkk[:], AF.Square, [bkk], [s_])
                        S.op("dve", lambda e, s_=s_, p=p: e.tensor_reduce(
                            out=ssk[:, p, :], in_=s_[:].rearrange("p (h d) -> p h d", h=4), axis=AX.X, op=ALU.add),
                            [s_], [ssk])
                        yield
                def kvB():
                    for hl in range(4):
                        h = 4 * hh + hl
                        for tc in range(8):
                            bk = banks[4 + (hl * 8 + tc) % 2]
                            for c in range(2):
                                mm(bk[:], Wkv[:, c, h * 256:h * 256 + 128], ckvT[:, c, tc * 512:(tc + 1) * 512],
                                   c == 0, c == 1, [Wkv] + ckvT_d[4 * tc:4 * tc + 4], [bk])
                            if (hl * 8 + tc) % 2:
                                act(KTn[:, hl, tc * 512:(tc + 1) * 512], bk[:], AF.Copy, [bk, gk_col], [KTn_d[hl][tc]],
                                    scale=gk_col[:, 0:1])
                            else:
                                ts(KTn[:, hl, tc * 512:(tc + 1) * 512], bk[:], gk_col[:, 0:1], None, ALU.mult, None,
                                   [bk, gk_col], [KTn_d[hl][tc]])
                            yield
                zip_gens(kvA(), kvB())
                tt(ksc[:], ssk[:], ssr[:].unsqueeze(2).to_broadcast([128, NB, 4]), ALU.add, [ssk, ssr], [ksc])
                act(ksc[:], ksc[:], AF.Ln, [ksc, epsc], [ksc], bias=epsc[:, 0:1], scale=1.0 / 192.0)
                act(ksc[:], ksc[:], AF.Exp, [ksc, lnsc], [ksc], bias=lnsc[:, 0:1], scale=-0.5)
                if hh == 0:
                    for i in range(4):
                        qbuild(0, 0, i, 0)
                for g in range(NGRP):
                    si = hh * NGRP + g
                    for hl in range(4):
                        attend(hh, g, hl, si % 2)
                        if si + 1 < len(seq):
                            qbuild(seq[si + 1][0], seq[si + 1][1], hl, (si + 1) % 2)
            if debug:
                S.barrier()
                d5 = dbg_out("d_OT", [128, 8, NOWN * 128], BF16)
                S.dma("sp", d5, OT, OT_d, [])
        A.pop()
        A.pop()
        S.barrier()

        if stage >= 3:
            A.push()
            cp_pref[0] = "act"
            tbanks[:] = [6, 7]
            NOPOOL = ("pe", "act", "dve", "sp")
            Gbuf = A.tile([128, D], F32)
            invR = A.tile([128, 128], F32)
            bload(invR, c_invr)
            Wple = A.tile([128, 2, D], BF16)
            S.dma("pool", Wple[:], w_ple.rearrange("(c p) n -> p c n", p=128), [], [Wple])
            state = [A.tile([128, 512], F32) for _ in range(8)]
            state_bf = [A.tile([128, 512], BF16) for _ in range(2)]
            for s_ in state:
                memset(s_[:], 0.0, [s_])
            slots = [A.tile([128, 8, 512], BF16) for _ in range(NSLOT)]
            sl_i = [0]

            def wslot(w_ap, k0, nk, c0, ncols=512):
                sl = slots[sl_i[0] % NSLOT]
                sl_i[0] += 1
                src = w_ap[k0 * 128:(k0 + nk) * 128, c0:c0 + ncols].rearrange("(c p) n -> p c n", p=128)
                S.dma("pool", sl[:, 0:nk, 0:ncols], src, [], [sl])
                return sl

            xown = [A.tile([128, D], F32) for _ in range(4)]
            junk = A.tile([128, D], BF16)
            xnb = [A.tile([128, D], BF16) for _ in range(2)]
            xnTo = [A.tile([128, 8, 128], BF16) for _ in range(4)]
            stt3 = [A.tile([128, 8], F32) for _ in range(2)]
            ost = [A.tile([128, 8], F32) for _ in range(2)]
            roT = A.view([128, 16, 512], BF16)
            roT_d = [Dep() for _ in range(4)]
            pbf = [A.tile([128, 256], BF16) for _ in range(2)]
            gam = [1.0 - 2.0 ** (-5 - h) for h in range(4)]
            gamC = [g_ ** 128 for g_ in gam]

            def proj_tm(w_ap, K, c0, ncols, lhs_fn, blocks, evac, bank_of):
                nkp = max(K // 1024, 1)
                kper = min(K // 128, 8)
                bks = {}
                for kp in range(nkp):
                    sl = wslot(w_ap, kp * 8, kper, c0, ncols)
                    for bi, b in enumerate(blocks):
                        if kp == 0:
                            bks[b] = banks[bank_of(bi)]
                        bk = bks[b]
                        for c in range(kper):
                            lap, ldeps = lhs_fn(b, kp * 8 + c)
                            mm(bk[:, 0:ncols], lap, sl[:, c, 0:ncols], kp == 0 and c == 0,
                               kp == nkp - 1 and c == kper - 1, list(ldeps) + [sl], [bk])
                        if kp == nkp - 1:
                            evac(b, bk)

            def sigmoid_from(bk, ncols, dst_t):
                act(dst_t[:, 0:ncols], bk[:, 0:ncols], AF.Exp, [bk], [dst_t], scale=-1.0)
                act(dst_t[:, 0:ncols], dst_t[:, 0:ncols], AF.Ln, [dst_t, onec], [dst_t], bias=onec[:, 0:1], scale=1.0)
                act(dst_t[:, 0:ncols], dst_t[:, 0:ncols], AF.Exp, [dst_t], [dst_t], scale=-1.0)

            def norm_to_T(b, G_src, dstT, dstT_d):
                st = stt3[b % 2]
                xn_ = xnb[b % 2]
                act(junk[:], xown[b][:], AF.Square, [xown[b]], [junk, st], scale=1.0 / 32.0, accum_out=st[:, 0:1])
                rstd_from_ms(st[:, 0:1], st[:, 0:1], [st], [st])
                stt(xn_[:], xown[b][:], st[:, 0:1], G_src[:], ALU.mult, ALU.mult, [xown[b], st, G_src], [xn_])
                transpose_to(dstT[:, :, b * 128:(b + 1) * 128], [xn_[:, c * 128:(c + 1) * 128] for c in range(8)],
                             128, 8, [xn_], [dstT_d[b]])

            for g in range(NGRP):
                S.barrier(NOPOOL)
                A.push()
                xnTt = [A.tile([128, 8, 128], BF16) for _ in range(4)]
                csR = A.tile([128, 2 * 8 * 128], F32)
                sinR = csR.ap[:, 0:1024].rearrange("p (a b) -> p a b", a=8)
                cosR = csR.ap[:, 1024:2048].rearrange("p (a b) -> p a b", a=8)
                A.push()
                xoth = [A.tile([128, D], F32) for _ in range(4)]
                xn8 = [A.tile([128, D], BF16) for _ in range(8)]
                kfR = A.tile([128, 2048], F32)
                kiR = A.tile([128, 2048], I32)
                bload(Gbuf, g_mix)
                for half in range(2):
                    o3 = csR.ap[:, half * 1024:(half + 1) * 1024].rearrange("p (a b) -> p a b", a=8)
                    tt(o3, post_f[:, 8 * g:8 * g + 8].unsqueeze(2).to_broadcast([128, 8, 128]),
                       invR[:].unsqueeze(1).to_broadcast([128, 8, 128]), ALU.mult, [post_f, invR], [csR])
                sincos(csR, kfR, kiR, [], 1024, do_sin=False)
                items = []
                for i in range(4):
                    items.append((xoth[i], xn8[2 * i], xnTt[i], xt[4 * g + i]))
                    items.append((xown[i], xn8[2 * i + 1], xnTo[i], xo[4 * g + i]))
                norm_batch(items, Gbuf, stt3[0], 0)
                act(csR[:], csR[:], AF.Sin, [csR], [csR])
                A.pop()
                S.barrier(NOPOOL)

                A.push()
                HB = []
                for _ in range(2):
                    HB.append(dict(
                        Kh=[A.tile([128, 256], BF16) for _ in range(8)],
                        Vv=[A.tile([128, 512], BF16) for _ in range(8)],
                        Qt=[A.tile([128, 256], BF16) for _ in range(4)],
                        Kt=[A.tile([128, 256], BF16) for _ in range(4)],
                        SG=[A.tile([128, 512], F32) for _ in range(4)]))
                Gr1 = A.tile([128, 512], F32)
                rA = [A.tile([128, 256], F32)] * 2
                rB = [A.tile([128, 256], F32)] * 2
                rC = [A.tile([128, 256], F32) for _ in range(2)]
                sig = [A.tile([128, 512], F32)] * 2
                QT = [A.tile([128, 2, 128], BF16) for _ in range(2)]
                KT = [A.tile([128, 2, 128], BF16) for _ in range(2)]
                scb = [A.tile([128, 128], BF16) for _ in range(2)]
                rog = [A.tile([128, 512], BF16) for _ in range(2)]

                def lhs_x(b, k):
                    i, own = b
                    t_ = (xnTo if own else xnTt)[i]
                    return t_[:, k, :], [t_]

                all8 = [(i, own) for i in range(4) for own in (0, 1)]
                own4 = [(i, 1) for i in range(4)]

                def proj_gen(h):
                    hb = HB[h % 2]
                    Kh, Vv, Qt, Kt, SG, Gr = hb["Kh"], hb["Vv"], hb["Qt"], hb["Kt"], hb["SG"], Gr1
                    bload(Gr, g_ret[0:1, h * 512:(h + 1) * 512])

                    def rope_evac(kind):
                        def ev(b, bk):
                            i, own = b
                            blk = 2 * i + own
                            a_, b_, c_ = rA[blk % 2], rB[blk % 2], rC[blk % 2]
                            x3 = bk[:, 0:256].rearrange("p (a f) -> p a f", a=2)
                            cb = cosR[:, blk, :].unsqueeze(1).to_broadcast([128, 2, 128])
                            sb_ = sinR[:, blk, :].unsqueeze(1).to_broadcast([128, 2, 128])
                            dcol = dec[:, h:h + 1] if kind == "q" else dec[:, 8 + h:9 + h]
                            stt(a_[:].rearrange("p (a f) -> p a f", a=2), x3, dcol, cb, ALU.mult, ALU.mult,
                                [bk, csR, dec], [a_])
                            stt(b_[:].rearrange("p (a f) -> p a f", a=2), x3, dcol, sb_, ALU.mult, ALU.mult,
                                [bk, csR, dec], [b_])
                            if kind == "q":
                                tt(Qt[i][:, 0:128], a_[:, 0:128], b_[:, 128:256], ALU.subtract, [a_, b_], [Qt[i]])
                                tt(Qt[i][:, 128:256], a_[:, 128:256], b_[:, 0:128], ALU.add, [a_, b_], [Qt[i]])
                            elif not own:
                                tt(Kh[blk][:, 0:128], a_[:, 0:128], b_[:, 128:256], ALU.subtract, [a_, b_], [Kh[blk]])
                                tt(Kh[blk][:, 128:256], a_[:, 128:256], b_[:, 0:128], ALU.add, [a_, b_], [Kh[blk]])
                            else:
                                tt(c_[:, 0:128], a_[:, 0:128], b_[:, 128:256], ALU.subtract, [a_, b_], [c_])
                                tt(c_[:, 128:256], a_[:, 128:256], b_[:, 0:128], ALU.add, [a_, b_], [c_])
                                act(Kh[blk][:], c_[:], AF.Copy, [c_], [Kh[blk]])
                                act(Kt[i][:], c_[:], AF.Copy, [c_], [Kt[i]], scale=float(gam[h] ** -128))
                        return ev

                    def v_evac(b, bk):
                        i, own = b
                        cp_alt(Vv[2 * i + own][:], bk[:], [bk], [Vv[2 * i + own]])

                    def g_evac(b, bk):
                        i, own = b
                        sg = sig[i % 2]
                        sigmoid_from(bk, 512, sg)
                        tt(sg[:], sg[:], bk[:], ALU.mult, [sg, bk], [sg])
                        tt(SG[i][:], sg[:], Gr[:], ALU.mult, [sg, Gr], [SG[i]])

                    for (c0, ncols, blks, ev) in ((OFF_RK + h * 256, 256, all8, rope_evac("k")),
                                                  (OFF_RQ + h * 256, 256, own4, rope_evac("q")),
                                                  (OFF_RV + h * 512, 512, all8, v_evac),
                                                  (OFF_RG + h * 512, 512, own4, g_evac)):
                        sl = wslot(w_in, 0, 8, c0, ncols)
                        for bi, b_ in enumerate(blks):
                            bk = banks[bi % 4]
                            for c in range(8):
                                lap, ldeps = lhs_x(b_, c)
                                mm(bk[:, 0:ncols], lap, sl[:, c, 0:ncols], c == 0, c == 7, list(ldeps) + [sl], [bk])
                            ev(b_, bk)
                            yield

                def ret_gen(h):
                    hb = HB[h % 2]
                    Kh, Vv, Qt, Kt, SG = hb["Kh"], hb["Vv"], hb["Qt"], hb["Kt"], hb["SG"]

                    def state_update(blk):
                        for c in range(2):
                            bk = banks[4 + c]
                            mm(bk[:], Kh[blk][:, c * 128:(c + 1) * 128], Vv[blk][:], True, True,
                               [Kh[blk], Vv[blk]], [bk])
                            s_ = state[h * 2 + c]
                            stt(s_[:], s_[:], gamC[h], bk[:], ALU.mult, ALU.add, [s_, bk], [s_])

                    for i in range(4):
                        state_update(2 * i)
                        blk = 2 * i + 1
                        for c in range(2):
                            cp_alt(state_bf[c][:], state[2 * h + c][:], [state[2 * h + c]], [state_bf[c]])
                        qT_, kT_ = QT[i % 2], KT[i % 2]
                        transpose_to(qT_[:], [Qt[i][:, c * 128:(c + 1) * 128] for c in range(2)], 128, 2, [Qt[i]], [qT_])
                        transpose_to(kT_[:], [Kt[i][:, c * 128:(c + 1) * 128] for c in range(2)], 128, 2, [Kt[i]], [kT_])
                        yield 2
                        bs = banks[4]
                        for c in range(2):
                            mm(bs[:, 0:128], kT_[:, c, :], qT_[:, c, :], c == 0, c == 1, [kT_, qT_], [bs])
                        sc = scb[i % 2]
                        tt(sc[:], bs[:, 0:128], tri[:], ALU.mult, [bs, tri], [sc])
                        bo = banks[5]
                        mm(bo[:], sc[:], Vv[blk][:], True, False, [sc, Vv[blk]], [bo])
                        for c in range(2):
                            mm(bo[:], qT_[:, c, :], state_bf[c][:], False, c == 1, [qT_, state_bf[c]], [bo])
                        ot = ost[i % 2]
                        act(junk[:, 0:512], bo[:], AF.Square, [bo], [junk, ot], scale=1.0 / math.sqrt(512.0),
                            accum_out=ot[:, 0:1])
                        rstd_from_ms(ot[:, 0:1], ot[:, 0:1], [ot], [ot])
                        rg_ = rog[i % 2]
                        stt(rg_[:], bo[:], ot[:, 0:1], SG[i][:], ALU.mult, ALU.mult, [bo, ot, SG[i]], [rg_])
                        yield 2
                        transpose_to(roT[:, 4 * h:4 * h + 4, i * 128:(i + 1) * 128],
                                     [rg_[:, c * 128:(c + 1) * 128] for c in range(4)], 128, 4, [rg_], [roT_d[i]])
                        yield 1
                        state_update(blk)
                        yield 1

                for _ in proj_gen(0):
                    pass
                for h in range(4):
                    pg_it = proj_gen(h + 1) if h < 3 else None
                    for n_ in ret_gen(h):
                        for _ in range(n_):
                            if pg_it is not None:
                                try:
                                    next(pg_it)
                                except StopIteration:
                                    pg_it = None
                    if pg_it is not None:
                        for _ in pg_it:
                            pass
                A.pop()
                A.pop()
                S.barrier(NOPOOL)

                A.push()
                mixed = [A.tile([128, D], F32) for _ in range(4)]
                mixb = [A.tile([128, D], BF16) for _ in range(2)]
                mixT = A.view([128, 8, 512], BF16)
                mixT_d = [Dep() for _ in range(4)]
                sig = [A.tile([128, 512], F32) for _ in range(2)]
                h2T = A.view([128, 8, 512], BF16)
                h2T_d = [Dep() for _ in range(4)]
                actT = A.view([128, 16, 512], BF16)
                actT_d = [Dep() for _ in range(16)]
                pT = [A.tile([128, 2, 128], BF16) for _ in range(4)]
                rl = [A.tile([128, 512], F32) for _ in range(2)]
                xn4 = [A.tile([128, D], BF16) for _ in range(4)]

                def lhs_ro(b, k):
                    return roT[:, k, b * 128:(b + 1) * 128], [roT_d[b]]

                def lhs_ot(b, k):
                    return OT[:, k, (4 * g + b) * 128:(4 * g + b + 1) * 128], [OT_d[g]]

                def lhs_xo(b, k):
                    return xnTo[b][:, k, :], [xnTo[b]]

                for (wbr, Kbr, lhs_br, goff, first) in ((w_br, 2048, lhs_ro, OFF_GR, True),
                                                        (w_bm, 1024, lhs_ot, OFF_GM, False)):
                    for ct in range(2):
                        held = {}

                        def hold(b, bk, held=held):
                            held[b] = bk
                        proj_tm(wbr, Kbr, ct * 512, 512, lhs_br, [0, 1, 2, 3], hold, lambda bi: bi)

                        def gate_evac(b, bk, ct=ct, first=first, held=held):
                            sg = sig[b % 2]
                            sigmoid_from(bk, 512, sg)
                            mx = mixed[b]
                            if first:
                                tt(mx[:, ct * 512:(ct + 1) * 512], sg[:], held[b][:], ALU.mult, [sg, held[b]], [mx])
                            else:
                                tt(sg[:], sg[:], held[b][:], ALU.mult, [sg, held[b]], [sg])
                                tt(mx[:, ct * 512:(ct + 1) * 512], mx[:, ct * 512:(ct + 1) * 512], sg[:], ALU.add,
                                   [mx, sg], [mx])
                        proj_tm(w_in, 1024, goff + ct * 512, 512, lhs_xo, [0, 1, 2, 3], gate_evac,
                                lambda bi: 4 + bi % 2)
                for b in range(4):
                    mb = mixb[b % 2]
                    cp("act", mb[:], mixed[b][:], [mixed[b]], [mb])
                    transpose_to(mixT[:, :, b * 128:(b + 1) * 128], [mb[:, c * 128:(c + 1) * 128] for c in range(8)],
                                 128, 8, [mb], [mixT_d[b]])

                def lhs_mix(b, k):
                    return mixT[:, k, b * 128:(b + 1) * 128], [mixT_d[b]]

                def res_evac(ct):
                    def ev(b, bk):
                        tt(xown[b][:, ct * 512:(ct + 1) * 512], xown[b][:, ct * 512:(ct + 1) * 512], bk[:], ALU.add,
                           [xown[b], bk], [xown[b]])
                    return ev

                for ct in range(2):
                    proj_tm(w_o, 1024, ct * 512, 512, lhs_mix, [0, 1, 2, 3], res_evac(ct), lambda bi: bi % 6)
                bload(Gbuf, g_mlp)
                norm_batch([(xown[b], xn4[b], T(h2T[:, :, b * 128:(b + 1) * 128], h2T_d[b]), None) for b in range(4)],
                           Gbuf, stt3[0], 0)

                def lhs_act(b, k):
                    return actT[:, k, b * 128:(b + 1) * 128], [actT_d[k]]

                for hf in range(2):
                    for ft in range(4):
                        sl = wslot(w_up, 0, 8, hf * 2048 + ft * 512, 512)
                        for fi in range(4):
                            f = ft * 4 + fi
                            bk = banks[4 + f % 2]
                            for c in range(8):
                                mm(bk[:], sl[:, c, fi * 128:(fi + 1) * 128], h2T[:, c, :], c == 0, c == 7,
                                   [sl] + h2T_d, [bk])
                            r_ = rl[f % 2]
                            act(r_[:], bk[:], AF.Relu, [bk], [r_])
                            act(actT[:, f, :], r_[:], AF.Square, [r_], [actT_d[f]])
                    for ct in range(2):
                        proj_tm(w_down[hf * 2048:(hf + 1) * 2048, :], 2048, ct * 512, 512, lhs_act, [0, 1, 2, 3],
                                res_evac(ct), lambda bi: bi)

                bload(Gbuf, g_ple)
                norm_batch([(xown[b], xn4[b], T(h2T[:, :, b * 128:(b + 1) * 128], h2T_d[b]), None) for b in range(4)],
                           Gbuf, stt3[1], 0)
                for b in range(4):
                    pb_ = pbf[b % 2]
                    S.dma("pool", pb_[:], pp[4 * g + b], [], [pb_])
                    transpose_to(pT[b][:], [pb_[:, c * 128:(c + 1) * 128] for c in range(2)], 128, 2, [pb_], [pT[b]])

                def lhs_h3(b, k):
                    return h2T[:, k, b * 128:(b + 1) * 128], [h2T_d[b]]

                for ct in range(2):
                    def ple_evac(b, bk, ct=ct):
                        sg = sig[b % 2]
                        sigmoid_from(bk, 512, sg)
                        bp = banks[4 + b % 2]
                        for c in range(2):
                            mm(bp[:], pT[b][:, c, :], Wple[:, c, ct * 512:(ct + 1) * 512], c == 0, c == 1,
                               [pT[b], Wple], [bp])
                        tt(sg[:], sg[:], bp[:], ALU.mult, [sg, bp], [sg])
                        tt(xown[b][:, ct * 512:(ct + 1) * 512], xown[b][:, ct * 512:(ct + 1) * 512], sg[:], ALU.add,
                           [xown[b], sg], [xown[b]])
                    proj_tm(w_pg, 1024, ct * 512, 512, lhs_h3, [0, 1, 2, 3], ple_evac, lambda bi: bi)
                for b in range(4):
                    S.dma("sp", out_d[4 * g + b], xown[b][:], [xown[b]], [])
                A.pop()
            A.pop()

        S.barrier()
        S.emit()
    return nc, dbg


def make_in_maps(x, p, positions, norm_mix, w_in, ret_norm, q_lat_norm, kv_lat_norm, w_uq, w_ukv,
                 q_norm, k_norm, w_br, w_bm, w_o, norm_mlp, w_up, w_down, norm_ple, w_ple_gate, w_ple):
    f32 = np.float32
    x = np.asarray(x, f32)
    p = np.asarray(p, f32)
    positions = np.asarray(positions, np.int32)
    shared = {
        "w_in": np.ascontiguousarray(np.asarray(w_in, f32)[0]),
        "w_uq": np.ascontiguousarray(np.asarray(w_uq, f32)[0]),
        "w_ukv": np.ascontiguousarray(np.asarray(w_ukv, f32)[0]),
        "w_br": np.ascontiguousarray(np.asarray(w_br, f32)[0]),
        "w_bm": np.ascontiguousarray(np.asarray(w_bm, f32)[0]),
        "w_o": np.ascontiguousarray(np.asarray(w_o, f32)[0]),
        "w_up": np.ascontiguousarray(np.asarray(w_up, f32)[0]),
        "w_down": np.ascontiguousarray(np.asarray(w_down, f32)[0]),
        "w_pg": np.ascontiguousarray(np.asarray(w_ple_gate, f32)[0]),
        "w_ple": np.ascontiguousarray(np.asarray(w_ple, f32)[0]),
        "g_mix": np.asarray(norm_mix, f32).reshape(1, -1),
        "g_ret": np.asarray(ret_norm, f32).reshape(1, -1),
        "g_ql": np.asarray(q_lat_norm, f32).reshape(1, -1),
        "g_kl": np.asarray(kv_lat_norm, f32).reshape(1, -1),
        "g_q": np.asarray(q_norm, f32).reshape(1, -1),
        "g_k": np.asarray(k_norm, f32).reshape(1, -1),
        "g_mlp": np.asarray(norm_mlp, f32).reshape(1, -1),
        "g_ple": np.asarray(norm_ple, f32).reshape(1, -1),
    }
    shared["c_ident"] = np.eye(128, dtype=f32)
    kk = np.arange(128)
    shared["c_tri"] = (kk[:, None] <= kk[None, :]).astype(f32)
    shared["c_invm"] = (f32(10000.0) ** (-np.arange(32, dtype=f32) / f32(32))).astype(f32).reshape(1, 32)
    shared["c_invr"] = (f32(10000.0) ** (-np.arange(128, dtype=f32) / f32(128))).astype(f32).reshape(1, 128)
    t = np.arange(128, dtype=np.float64)
    dec = np.zeros((128, 12), np.float64)
    for h in range(4):
        gmm = 1.0 - 2.0 ** (-5 - h)
        dec[:, h] = gmm ** (t + 1.0)
        dec[:, 4 + h] = gmm ** (-(t + 1.0)) / 16.0
        dec[:, 8 + h] = gmm ** (127.0 - t) / 16.0
    shared["c_dec"] = dec.astype(f32)
    maps = []
    for c in range(8):
        b, j = c // 2, c % 2
        xb = x[b].reshape(NB, 128, D)
        pb = positions[b].reshape(NB, 128)
        xo_ = np.ascontiguousarray(xb[j::2])
        po = pb[j::2]
        kb = np.zeros((128, NB), f32)
        if j == 1:
            xt_ = np.ascontiguousarray(xb[0::2])
            pt = pb[0::2]
        else:
            xt_ = np.concatenate([np.zeros((1, 128, D), f32), xb[1:NB - 1:2]], axis=0)
            pt = np.concatenate([np.zeros((1, 128), np.int32), pb[1:NB - 1:2]], axis=0)
            kb[:, 0] = -30000.0
        post = np.zeros((128, NB), np.int32)
        post[:, 0::2] = pt.T
        post[:, 1::2] = po.T
        m = dict(shared)
        m["xo"] = xo_
        m["xt"] = np.ascontiguousarray(xt_)
        m["pp"] = np.ascontiguousarray(p[0, b].reshape(NB, 128, 256)[j::2])
        m["post"] = post
        m["kbias"] = kb
        maps.append(m)
    return maps


_CACHE = {}


def kernel(**inputs):
    if "nc" not in _CACHE:
        _CACHE["nc"] = build_program()[0]
    nc = _CACHE["nc"]
    maps = make_in_maps(**inputs)
    res = run_bass_kernel_spmd(nc, maps, core_ids=list(range(8)))
    out = np.zeros((4, NB, 128, D), np.float32)
    for c in range(8):
        b, j = c // 2, c % 2
        out[b, j::2] = np.asarray(res.results[c]["out"]).reshape(NOWN, 128, D)
    return out.reshape(4, NB * 128, D)
```

```python
import contextlib
import math

import numpy as np
import concourse.bass as bass
import concourse.mybir as mybir
from concourse.bass_utils import run_bass_kernel_spmd

F32 = mybir.dt.float32
BF16 = mybir.dt.bfloat16
I32 = mybir.dt.int32
AF = mybir.ActivationFunctionType
ALU = mybir.AluOpType
AX = mybir.AxisListType

D = 1024
NB = 32
NOWN = 16
EPS = 1e-6
OFF_RQ, OFF_RK, OFF_RV, OFF_RG, OFF_CQ, OFF_CKV, OFF_KR, OFF_GR, OFF_GM = (
    0, 1024, 2048, 4096, 6144, 6528, 6784, 6848, 7872)
NGRP = 4
TWO_PI = 2.0 * math.pi
C1 = 6.28125
C2 = TWO_PI - C1


class Dep:
    __slots__ = ("w", "r")

    def __init__(self):
        self.w = None
        self.r = []


class T:
    __slots__ = ("ap", "d")

    def __init__(self, ap, d=None):
        self.ap = ap
        self.d = d if d is not None else Dep()

    def __getitem__(self, k):
        return self.ap[k]


class Sched:
    ENGS = ("pe", "act", "dve", "pool", "sp")

    def __init__(self, nc, es, n_dma_sems=8):
        self.nc = nc
        self.ops = {e: [] for e in self.ENGS}
        self.cnt = {e: 0 for e in self.ENGS}
        self.waited = {e: {} for e in self.ENGS}
        self.semobj = {}
        for e in self.ENGS:
            self.semobj[("e", e)] = es.enter_context(nc.semaphore("sem_" + e))
        self.dq = {}
        for q in ("sp", "pool"):
            n = n_dma_sems
            for i in range(n):
                self.semobj[("d", q, i)] = es.enter_context(nc.semaphore("dq_%s_%d" % (q, i)))
            self.dq[q] = {"n": [0] * n, "next": 0}

    def _need(self, eng, ev, waits):
        if ev is None:
            return
        key, val = ev
        if eng == "pe" and key == ("e", "pe"):
            return
        if self.waited[eng].get(key, 0) >= val:
            return
        self.waited[eng][key] = val
        waits.append((key, val))

    def _deps(self, eng, reads, writes):
        waits = []
        for d in reads:
            self._need(eng, d.w, waits)
        for d in writes:
            self._need(eng, d.w, waits)
            for ev in d.r:
                self._need(eng, ev, waits)
        return waits

    @staticmethod
    def _commit(ev, reads, writes):
        for d in reads:
            d.r.append(ev)
        for d in writes:
            d.w = ev
            d.r = []

    def op(self, eng, fn, reads=(), writes=()):
        reads = [t.d if isinstance(t, T) else t for t in reads]
        writes = [t.d if isinstance(t, T) else t for t in writes]
        waits = self._deps(eng, reads, writes)
        self.cnt[eng] += 1
        ev = (("e", eng), self.cnt[eng])
        self.ops[eng].append((waits, fn, (("e", eng), 1)))
        self._commit(ev, reads, writes)

    def dma(self, q, out, in_, reads=(), writes=()):
        reads = [t.d if isinstance(t, T) else t for t in reads]
        writes = [t.d if isinstance(t, T) else t for t in writes]
        d = self.dq[q]
        i = d["next"]
        d["next"] = (i + 1) % len(d["n"])
        key = ("d", q, i)
        waits = self._deps(q, reads, writes)
        if d["n"][i]:
            self._need(q, (key, 16 * d["n"][i]), waits)
        d["n"][i] += 1
        ev = (key, 16 * d["n"][i])
        self.ops[q].append((waits, lambda e, o=out, s=in_: e.dma_start(out=o, in_=s), (key, 16)))
        self._commit(ev, reads, writes)

    def all_events(self):
        evs = [(("e", e), self.cnt[e]) for e in self.ENGS if self.cnt[e]]
        for q, d in self.dq.items():
            evs += [(("d", q, i), 16 * n) for i, n in enumerate(d["n"]) if n]
        return evs

    def barrier(self, engines=None):
        evs = self.all_events()
        for e in (engines or self.ENGS):
            waits = []
            for ev in evs:
                self._need(e, ev, waits)
            if waits:
                self.ops[e].append((waits, None, None))

    def emit(self):
        nc = self.nc
        so = self.semobj

        def run(e, name):
            for waits, fn, inc in self.ops[name]:
                for key, val in waits:
                    e.wait_ge(so[key], val)
                if fn is not None:
                    fn(e).then_inc(so[inc[0]], inc[1])

        with nc.Block() as block:
            @block.tensor
            def _(e):
                run(e, "pe")

            @block.scalar
            def _(e):
                run(e, "act")

            @block.vector
            def _(e):
                run(e, "dve")

            @block.gpsimd
            def _(e):
                run(e, "pool")

            @block.sync
            def _(e):
                run(e, "sp")


def build_program(stage=99, debug=False):
    nc = bass.Bass("TRN2", target_bir_lowering=False)

    def din(name, shape, dt=F32):
        return nc.dram_tensor(name, list(shape), dt, kind="ExternalInput").ap()

    xo = din("xo", [NOWN, 128, D])
    xt = din("xt", [NOWN, 128, D])
    pp = din("pp", [NOWN, 128, 256])
    post = din("post", [128, NB], I32)
    kbias_d = din("kbias", [128, NB])
    w_in = din("w_in", [D, 8896])
    w_uq = din("w_uq", [384, 1536])
    w_ukv = din("w_ukv", [256, 2048])
    w_br = din("w_br", [2048, D])
    w_bm = din("w_bm", [D, D])
    w_o = din("w_o", [D, D])
    w_up = din("w_up", [D, 4096])
    w_down = din("w_down", [4096, D])
    w_pg = din("w_pg", [D, D])
    w_ple = din("w_ple", [256, D])
    g_mix = din("g_mix", [1, D])
    g_ret = din("g_ret", [1, 2048])
    g_ql = din("g_ql", [1, 384])
    g_kl = din("g_kl", [1, 256])
    g_q = din("g_q", [1, 192])
    g_k = din("g_k", [1, 192])
    g_mlp = din("g_mlp", [1, D])
    g_ple = din("g_ple", [1, D])
    c_ident = din("c_ident", [128, 128])
    c_tri = din("c_tri", [128, 128])
    c_invm = din("c_invm", [1, 32])
    c_invr = din("c_invr", [1, 128])
    c_dec = din("c_dec", [128, 12])
    out_d = nc.dram_tensor("out", [NOWN, 128, D], F32, kind="ExternalOutput").ap()
    dbg = {}

    def dbg_out(name, shape, dt=F32):
        dbg[name] = nc.dram_tensor(name, list(shape), dt, kind="ExternalOutput").ap()
        return dbg[name]

    with contextlib.ExitStack() as es:
        S = Sched(nc, es)
        ARENA_COLS = 106300
        arena = es.enter_context(nc.sbuf_tensor("arena", [128, ARENA_COLS], BF16))
        banks = [T(es.enter_context(nc.psum_tensor("bank%d" % i, [128, 512], F32))[:]) for i in range(8)]

        class Alloc:
            def __init__(self):
                self.top = 0
                self.marks = []

            def view(self, shape, dt, parts=128):
                n = 1
                for s in shape[1:]:
                    n *= s
                cols = n if dt == BF16 else 2 * n
                cols = (cols + 15) // 16 * 16
                assert self.top + cols <= ARENA_COLS, ("SBUF arena overflow", self.top, cols, shape)
                ap = arena[0:shape[0], self.top:self.top + (n if dt == BF16 else 2 * n)]
                self.top += cols
                if dt != BF16:
                    ap = ap.bitcast(dt)
                if len(shape) == 3:
                    ap = ap.rearrange("p (a b) -> p a b", a=shape[1])
                elif len(shape) == 4:
                    ap = ap.rearrange("p (a b c) -> p a b c", a=shape[1], b=shape[2])
                return ap

            def tile(self, shape, dt):
                return T(self.view(shape, dt))

            def push(self):
                self.marks.append(self.top)

            def pop(self):
                self.top = self.marks.pop()

        A = Alloc()

        def act(out, in_, func, r, w, **kw):
            S.op("act", lambda e: e.activation(out=out, in_=in_, func=func, **kw), r, w)

        def tt(out, in0, in1, op, r, w, eng="dve"):
            S.op(eng, lambda e: e.tensor_tensor(out=out, in0=in0, in1=in1, op=op), r, w)

        def ts(out, in0, s1, s2, op0, op1, r, w, eng="dve"):
            if s2 is None:
                S.op(eng, lambda e: e.tensor_scalar(out=out, in0=in0, scalar1=s1, scalar2=None, op0=op0), r, w)
            else:
                S.op(eng, lambda e: e.tensor_scalar(out=out, in0=in0, scalar1=s1, scalar2=s2, op0=op0, op1=op1), r, w)

        def stt(out, in0, scalar, in1, op0, op1, r, w, eng="dve"):
            S.op(eng, lambda e: e.scalar_tensor_tensor(out=out, in0=in0, scalar=scalar, in1=in1, op0=op0, op1=op1), r, w)

        def cp(eng, out, in_, r, w):
            if eng == "act":
                S.op("act", lambda e: e.copy(out=out, in_=in_), r, w)
            else:
                S.op(eng, lambda e: e.tensor_copy(out=out, in_=in_), r, w)

        def mm(out, lhsT, rhs, start, stop, r, w):
            S.op("pe", lambda e: e.matmul(out, lhsT=lhsT, rhs=rhs, start=start, stop=stop), r, w)

        def memset(ap, val, w, eng="dve"):
            S.op(eng, lambda e: e.memset(ap, val), [], w)

        cpi = [0]
        cp_pref = ["alt"]

        def cp_alt(out, in_, r, w):
            cpi[0] += 1
            if cp_pref[0] == "alt":
                cp("act" if cpi[0] % 2 else "dve", out, in_, r, w)
            else:
                cp(cp_pref[0], out, in_, r, w)

        ident = A.tile([128, 128], BF16)
        tri = A.tile([128, 128], BF16)
        ones = A.tile([128, 128], BF16)
        epsc = A.tile([128, 1], F32)
        post_i = A.tile([128, NB], I32)
        post_f = A.tile([128, NB], F32)
        kbias = A.tile([128, NB], F32)
        dec = A.tile([128, 12], F32)
        S.dma("pool", ident[:], c_ident, [], [ident])
        S.dma("pool", tri[:], c_tri, [], [tri])
        S.dma("sp", post_i[:], post, [], [post_i])
        S.dma("sp", kbias[:], kbias_d, [], [kbias])
        S.dma("sp", dec[:], c_dec, [], [dec])
        memset(ones[:], 1.0, [ones])
        memset(epsc[:], EPS, [epsc])
        onec = A.tile([128, 1], F32)
        memset(onec[:], 1.0, [onec])
        cp("dve", post_f[:], post_i[:], [post_i], [post_f])

        trn = [0]
        tbanks = [6, 7]

        def transpose_to(dst_ap, src_ap, ncols_in, nblk, r_src, w_dst, in_parts=128, evac="alt"):
            bk = banks[tbanks[trn[0] % len(tbanks)]]
            trn[0] += 1
            pv = bk.ap.bitcast(BF16).rearrange("p (a b) -> p a b", a=8)
            for i in range(nblk):
                S.op("pe", lambda e, i=i: e.transpose(out=pv[0:ncols_in, i, :], in_=src_ap[i], identity=ident[:]),
                     list(r_src) + [ident], [bk])
            if evac == "alt":
                cp_alt(dst_ap, pv[0:ncols_in, 0:nblk, :], [bk], w_dst)
            else:
                cp(evac, dst_ap, pv[0:ncols_in, 0:nblk, :], [bk], w_dst)

        def rstd_from_ms(rstd_ap, ms_ap, r, w):
            act(rstd_ap, ms_ap, AF.Ln, r + [epsc], w, bias=epsc[:, 0:1], scale=1.0)
            act(rstd_ap, rstd_ap, AF.Exp, w, w, scale=-0.5)

        def sincos(cs, kf, ki, r, n, do_sin=True):
            ts(cs[:, n:2 * n], cs[:, n:2 * n], 0.5 * math.pi, None, ALU.add, None, [cs], [cs])
            ts(kf[:], cs[:], 1.0 / TWO_PI, None, ALU.mult, None, [cs], [kf])
            cp("dve", ki[:], kf[:], [kf], [ki])
            cp("dve", kf[:], ki[:], [ki], [kf])
            stt(cs[:], kf[:], -C1, cs[:], ALU.mult, ALU.add, [kf, cs], [cs])
            stt(cs[:], kf[:], -C2, cs[:], ALU.mult, ALU.add, [kf, cs], [cs])
            ts(kf[:], cs[:], math.pi, -TWO_PI, ALU.is_gt, ALU.mult, [cs], [kf])
            tt(cs[:], cs[:], kf[:], ALU.add, [cs, kf], [cs])
            if do_sin:
                act(cs[:], cs[:], AF.Sin, [cs], [cs])

        def bload(dst, src_row, eng="sp"):
            S.dma(eng, dst[:], src_row.partition_broadcast(128), [], [dst])

        NSLOT = 3
        OT = A.view([128, 8, NOWN * 128], BF16)
        OT_d = [Dep() for _ in range(NGRP)]
        A.push()

        ckvT = A.view([128, 2, NB * 128], BF16)
        ckvT_d = [Dep() for _ in range(NB)]
        KTr = A.view([128, NB * 128], BF16)
        KTr_z = Dep()
        S.op("dve", lambda e: e.memset(KTr[64:128, :], 0.0), [], [KTr_z])
        KTr_d = [Dep() for _ in range(NB)]
        cqT = A.view([128, 3, NOWN * 128], BF16)
        cqT_d = [Dep() for _ in range(NOWN)]
        ssr = A.tile([128, NB], F32)
        csM = A.tile([128, 2 * NB * 32], F32)
        sinM = csM.ap[:, 0:NB * 32].rearrange("p (a b) -> p a b", a=NB)
        cosM = csM.ap[:, NB * 32:2 * NB * 32].rearrange("p (a b) -> p a b", a=NB)
        G_k = A.tile([128, 192], F32)
        G_q = A.tile([128, 192], F32)
        gk_col = A.tile([128, 1], F32)
        bload(G_k, g_k)
        bload(G_q, g_q)
        S.dma("sp", gk_col[:], g_k[0:1, 0:128].rearrange("o d -> d o"), [], [gk_col])

        A.push()
        G_mix = A.tile([128, D], F32)
        G_ql = A.tile([128, 384], F32)
        G_kl = A.tile([128, 256], F32)
        invM = A.tile([128, 32], F32)
        bload(G_mix, g_mix)
        bload(G_ql, g_ql)
        bload(G_kl, g_kl)
        bload(invM, c_invm)
        WA = A.tile([128, 8, 320], BF16)
        WB = A.tile([128, 8, 384], BF16)
        w_in_v = w_in.rearrange("(c p) n -> p c n", p=128)
        S.dma("pool", WA[:], w_in_v[:, :, OFF_CKV:OFF_CKV + 320], [], [WA])
        S.dma("pool", WB[:], w_in_v[:, :, OFF_CQ:OFF_CQ + 384], [], [WB])
        kfM = A.tile([128, 2 * NB * 32], F32)
        kiM = A.tile([128, 2 * NB * 32], I32)
        for half in range(2):
            o3 = csM.ap[:, half * NB * 32:(half + 1) * NB * 32].rearrange("p (a b) -> p a b", a=NB)
            tt(o3, post_f[:].unsqueeze(2).to_broadcast([128, NB, 32]),
               invM[:].unsqueeze(1).to_broadcast([128, NB, 32]), ALU.mult, [post_f, invM], [csM])
        sincos(csM, kfM, kiM, [], NB * 32)

        junk = A.tile([128, D], BF16)

        def norm_batch_gen(items, G, st, c0):
            n = len(items)
            for k, (xb, xn_, xT, src) in enumerate(items):
                if src is not None:
                    S.dma("sp", xb[:], src, [], [xb])
                act(junk[:], xb[:], AF.Square, [xb], [junk, st], scale=1.0 / 32.0, accum_out=st[:, c0 + k:c0 + k + 1])
                yield
            rstd_from_ms(st[:, c0:c0 + n], st[:, c0:c0 + n], [st], [st])
            yield
            for k, (xb, xn_, xT, src) in enumerate(items):
                stt(xn_[:], xb[:], st[:, c0 + k:c0 + k + 1], G[:], ALU.mult, ALU.mult, [xb, st, G], [xn_])
                yield
            for k, (xb, xn_, xT, src) in enumerate(items):
                transpose_to(xT[:], [xn_[:, c * 128:(c + 1) * 128] for c in range(8)], 128, 8, [xn_], [xT])
                yield

        def norm_batch(items, G, st, c0):
            for _ in norm_batch_gen(items, G, st, c0):
                pass

        NBT = 4
        xb1 = [[A.tile([128, D], F32) for _ in range(NBT)] for _ in range(2)]
        xn1 = [[A.tile([128, D], BF16) for _ in range(NBT)] for _ in range(2)]
        xT1 = [[A.tile([128, 8, 128], BF16) for _ in range(NBT)] for _ in range(2)]
        st1 = [A.tile([128, 16], F32) for _ in range(2)]
        for t_ in st1:
            memset(t_[:], 1.0, [t_])
        ckvn = [A.tile([128, 256], BF16) for _ in range(NBT)]
        cqn = [A.tile([128, 384], BF16) for _ in range(NBT)]
        krg = [A.tile([128, 64], F32) for _ in range(NBT)]
        krA = [A.tile([128, 64], F32) for _ in range(NBT)]
        krB = [A.tile([128, 64], F32) for _ in range(NBT)]
        krr = [A.tile([128, 64], BF16) for _ in range(NBT)]

        nb1 = NB if stage >= 1 else 0

        def p1_norm(p0):
                par = (p0 // NBT) % 2
                st = st1[par]
                ps = list(range(p0, p0 + NBT))
                yield from norm_batch_gen([(xb1[par][k], xn1[par][k], xT1[par][k], (xo if p % 2 else xt)[p // 2])
                                       for k, p in enumerate(ps)], G_mix, st, 0)

        def p1_latent(p0):
                par = (p0 // NBT) % 2
                st = st1[par]
                ps = list(range(p0, p0 + NBT))
                bkB_of = {}
                for k, p in enumerate(ps):
                    xT = xT1[par][k]
                    bkA = banks[k]
                    for c in range(8):
                        mm(bkA[:, 0:320], xT[:, c, :], WA[:, c, :], c == 0, c == 7, [xT, WA], [bkA])
                    if p % 2 == 1:
                        bkB = banks[4 + (k // 2) % 2]
                        bkB_of[k] = bkB
                        for c in range(8):
                            mm(bkB[:, 0:384], xT[:, c, :], WB[:, c, :], c == 0, c == 7, [xT, WB], [bkB])
                    yield
                for k, p in enumerate(ps):
                    bkA = banks[k]
                    act(junk[:, 0:256], bkA[:, 0:256], AF.Square, [bkA], [junk, st], scale=1.0 / 16.0,
                        accum_out=st[:, 4 + k:5 + k])
                    act(junk[:, 256:320], bkA[:, 256:320], AF.Square, [bkA], [junk, ssr], accum_out=ssr[:, p:p + 1])
                    if p % 2 == 1:
                        act(junk[:, 0:384], bkB_of[k][:, 0:384], AF.Square, [bkB_of[k]], [junk, st],
                            scale=1.0 / math.sqrt(384.0), accum_out=st[:, 8 + k:9 + k])
                    yield
                rstd_from_ms(st[:, 4:12], st[:, 4:12], [st], [st])
                yield
                for k, p in enumerate(ps):
                    bkA = banks[k]
                    cn = ckvn[k]
                    stt(cn[:], bkA[:, 0:256], st[:, 4 + k:5 + k], G_kl[:], ALU.mult, ALU.mult, [bkA, st, G_kl], [cn])
                    kg, kA, kB, kr = krg[k], krA[k], krB[k], krr[k]
                    tt(kg[:], bkA[:, 256:320], G_k[:, 128:192], ALU.mult, [bkA, G_k], [kg])
                    kg3 = kg[:].rearrange("p (a b) -> p a b", a=2)
                    tt(kA[:].rearrange("p (a b) -> p a b", a=2), kg3,
                       cosM[:, p, :].unsqueeze(1).to_broadcast([128, 2, 32]), ALU.mult, [kg, csM], [kA])
                    tt(kB[:].rearrange("p (a b) -> p a b", a=2), kg3,
                       sinM[:, p, :].unsqueeze(1).to_broadcast([128, 2, 32]), ALU.mult, [kg, csM], [kB])
                    tt(kr[:, 0:32], kA[:, 0:32], kB[:, 32:64], ALU.subtract, [kA, kB], [kr])
                    tt(kr[:, 32:64], kA[:, 32:64], kB[:, 0:32], ALU.add, [kA, kB], [kr])
                    if p % 2 == 1:
                        qn_ = cqn[k]
                        stt(qn_[:], bkB_of[k][:, 0:384], st[:, 8 + k:9 + k], G_ql[:], ALU.mult, ALU.mult,
                            [bkB_of[k], st, G_ql], [qn_])
                    yield
                for k, p in enumerate(ps):
                    m = p // 2
                    cn, kr = ckvn[k], krr[k]
                    transpose_to(ckvT[:, :, p * 128:(p + 1) * 128], [cn[:, c * 128:(c + 1) * 128] for c in range(2)],
                                 128, 2, [cn], [ckvT_d[p]])
                    transpose_to(KTr[0:64, p * 128:(p + 1) * 128].unsqueeze(1), [kr[:, 0:64]], 64, 1, [kr], [KTr_d[p]])
                    if p % 2 == 1:
                        qn_ = cqn[k]
                        transpose_to(cqT[:, :, m * 128:(m + 1) * 128], [qn_[:, c * 128:(c + 1) * 128] for c in range(3)],
                                     128, 3, [qn_], [cqT_d[m]])
                    yield

        def zip_gens(ga, gb):
            da = db = False
            while not (da and db):
                if not da:
                    try:
                        next(ga)
                    except StopIteration:
                        da = True
                if not db:
                    try:
                        next(gb)
                    except StopIteration:
                        db = True

        p0s = list(range(0, nb1, NBT))
        if p0s:
            for _ in p1_norm(p0s[0]):
                pass
        for n_, p0 in enumerate(p0s):
            nxt = p1_norm(p0s[n_ + 1]) if n_ + 1 < len(p0s) else iter(())
            zip_gens(p1_latent(p0), nxt)
        if debug and stage >= 1:
            S.barrier()
            d1 = dbg_out("d_ckvT", [128, 2, NB * 128], BF16)
            S.dma("sp", d1, ckvT, ckvT_d, [])
            d2 = dbg_out("d_KTr", [64, NB * 128], BF16)
            S.dma("sp", d2, KTr[0:64, :], KTr_d, [])
            d3 = dbg_out("d_cqT", [128, 3, NOWN * 128], BF16)
            S.dma("sp", d3, cqT, cqT_d, [])
            d4 = dbg_out("d_ssr", [128, NB])
            S.dma("sp", d4, ssr[:], [ssr], [])
        A.pop()
        S.barrier()

        A.push()
        if stage >= 2:
            Wkv = A.tile([128, 2, 2048], BF16)
            Wq = A.tile([128, 3, 1536], BF16)
            S.dma("pool", Wkv[:], w_ukv.rearrange("(c p) n -> p c n", p=128), [], [Wkv])
            S.dma("pool", Wq[:], w_uq.rearrange("(c p) n -> p c n", p=128), [], [Wq])
            Wkv4 = Wkv[:].rearrange("p c (h t d) -> p c h t d", t=2, d=128)
            KTn = A.view([128, 4, NB * 128], BF16)
            KTn_d = [[Dep() for _ in range(8)] for _ in range(4)]
            Vt = A.view([128, NB, 4, 128], BF16)
            Vt_d = [Dep() for _ in range(NB)]
            ssk = A.tile([128, NB, 4], F32)
            ksc = A.tile([128, NB, 4], F32)
            sq = [A.tile([128, 512], F32) for _ in range(2)]
            QTn = [A.view([128, 4, 512], BF16) for _ in range(2)]
            QTn_d = [[Dep() for _ in range(4)] for _ in range(2)]
            QTr = [A.view([128, 4, 512], BF16) for _ in range(2)]
            QTr_z = Dep()
            for q_ in QTr:
                S.op("dve", lambda e, q_=q_: e.memset(q_[64:128, :, :], 0.0), [], [QTr_z])
            QTr_d = [[Dep() for _ in range(4)] for _ in range(2)]
            qst = [A.tile([128, 8], F32) for _ in range(2)]
            qn32 = [A.tile([128, 4, 192], F32) for _ in range(2)]
            qA = [A.tile([128, 4, 64], F32) for _ in range(2)]
            qB = [A.tile([128, 4, 64], F32) for _ in range(2)]
            qbf = [A.tile([128, 4, 192], BF16) for _ in range(2)]
            NPB = 4
            Pb = [A.tile([128, 512], BF16) for _ in range(NPB)]
            rec = A.tile([128, 512], F32)
            lnsc = A.tile([128, 1], F32)
            memset(lnsc[:], -0.5 * math.log(192.0), [lnsc])
            pbi = 0
            sbi = 0
            tbanks[:] = [7]
            SBK = [0, 1, 6]
            seq = [(h_, g_) for h_ in range(2) for g_ in range(NGRP)]

            def qbuild(hh, g, i, qb):
                m = 4 * g + i
                p_own = 2 * m + 1
                st = qst[i % 2]
                q32, qa, qb_, qh = qn32[i % 2], qA[i % 2], qB[i % 2], qbf[i % 2]
                for half in range(2):
                    bk = banks[(2 * i + half) % 2]
                    c0 = (4 * hh + 2 * half) * 192
                    for c in range(3):
                        mm(bk[:, 0:384], cqT[:, c, m * 128:(m + 1) * 128], Wq[:, c, c0:c0 + 384],
                           c == 0, c == 2, [cqT_d[m], Wq], [bk])
                    s_ = sq[half]
                    act(s_[:, 0:384], bk[:, 0:384], AF.Square, [bk], [s_], scale=1.0 / math.sqrt(192.0))
                    S.op("dve", lambda e, s_=s_, st=st, half=half: e.tensor_reduce(
                        out=st[:, 2 * half:2 * half + 2], in_=s_[:, 0:384].rearrange("p (h d) -> p h d", h=2),
                        axis=AX.X, op=ALU.add), [s_], [st])
                    rstd_from_ms(st[:, 2 * half:2 * half + 2], st[:, 2 * half:2 * half + 2], [st], [st])
                    for hq in range(2):
                        stt(q32[:, 2 * half + hq, :], bk[:, hq * 192:(hq + 1) * 192],
                            st[:, 2 * half + hq:2 * half + hq + 1], G_q[:], ALU.mult, ALU.mult,
                            [bk, st, G_q], [q32])
                qr4 = q32[:, :, 128:192].rearrange("p h (a b) -> p h a b", a=2)
                tt(qa[:].rearrange("p h (a b) -> p h a b", a=2), qr4,
                   cosM[:, p_own, :].unsqueeze(1).unsqueeze(1).to_broadcast([128, 4, 2, 32]), ALU.mult,
                   [q32, csM], [qa])
                tt(qb_[:].rearrange("p h (a b) -> p h a b", a=2), qr4,
                   sinM[:, p_own, :].unsqueeze(1).unsqueeze(1).to_broadcast([128, 4, 2, 32]), ALU.mult,
                   [q32, csM], [qb_])
                tt(qh[:, :, 128:160], qa[:, :, 0:32], qb_[:, :, 32:64], ALU.subtract, [qa, qb_], [qh])
                tt(qh[:, :, 160:192], qa[:, :, 32:64], qb_[:, :, 0:32], ALU.add, [qa, qb_], [qh])
                cp("act", qh[:, :, 0:128], q32[:, :, 0:128], [q32], [qh])
                transpose_to(QTn[qb][:, :, i * 128:(i + 1) * 128], [qh[:, hl, 0:128] for hl in range(4)],
                             128, 4, [qh], [QTn_d[qb][i]])
                transpose_to(QTr[qb][0:64, :, i * 128:(i + 1) * 128], [qh[:, hl, 128:192] for hl in range(4)],
                             64, 4, [qh], [QTr_d[qb][i]])


            def attend(hh, g, hl, qb):
                nonlocal pbi, sbi
                h = 4 * hh + hl
                bO = banks[2 + hl % 2]
                bL = banks[4 + hl % 2]
                plist = list(range(8 * g + 8))
                pend = []

                def finish(item, first, last):
                    p_, i0_, Pt_ = item
                    n0 = i0_ * 128
                    mm(bO[:, n0:512], Vt[:, p_, hl, :], Pt_[:, n0:512], first, last, [Vt_d[p_], Pt_], [bO])
                    mm(bL[:, n0:512], ones[:], Pt_[:, n0:512], first, last, [ones, Pt_], [bL])

                for idx, p in enumerate(plist):
                    r = p - 8 * g
                    i0 = 0 if r <= 1 else r // 2
                    n0 = i0 * 128
                    bS = banks[SBK[sbi % 3]]
                    sbi += 1
                    mm(bS[:, n0:512], KTn[:, hl, p * 128:(p + 1) * 128], QTn[qb][:, hl, n0:512], True, False,
                       [KTn_d[hl][p // 4]] + QTn_d[qb][i0:4], [bS])
                    mm(bS[:, n0:512], KTr[:, p * 128:(p + 1) * 128], QTr[qb][:, hl, n0:512], False, True,
                       [KTr_d[p], KTr_z, QTr_z] + QTr_d[qb][i0:4], [bS])
                    Pt = Pb[pbi % NPB]
                    pbi += 1
                    act(Pt[:, n0:512], bS[:, n0:512], AF.Exp, [bS, ksc, kbias], [Pt],
                        bias=kbias[:, p:p + 1], scale=ksc[:, p, hl:hl + 1])
                    if r >= 1 and r % 2 == 1:
                        tt(Pt[:, n0:n0 + 128], Pt[:, n0:n0 + 128], tri[:], ALU.mult, [Pt, tri], [Pt])
                    pend.append((p, i0, Pt))
                    if len(pend) > 2:
                        it_ = pend.pop(0)
                        finish(it_, it_[0] == 0, False)
                while pend:
                    it_ = pend.pop(0)
                    finish(it_, it_[0] == 0, len(pend) == 0)
                S.op("dve", lambda e, bL=bL: e.reciprocal(out=rec[:], in_=bL[:]), [bL], [rec])
                tt(OT[:, h, g * 512:(g + 1) * 512], bO[:], rec[:], ALU.mult, [bO, rec], [OT_d[g]])

            for hh in range(2):
                def kvA():
                    for p in range(NB):
                        bkv = banks[(2 * p) % 4]
                        bkk = banks[(2 * p + 1) % 4]
                        for c in range(2):
                            mm(bkv[:].rearrange("p (h d) -> p h d", h=4), ckvT[:, c, p * 128:(p + 1) * 128],
                               Wkv4[:, c, 4 * hh:4 * hh + 4, 1, :], c == 0, c == 1, [ckvT_d[p], Wkv], [bkv])
                        cp_alt(Vt[:, p, :, :], bkv[:].rearrange("p (h d) -> p h d", h=4), [bkv], [Vt_d[p]])
                        for c in range(2):
                            mm(bkk[:].rearrange("p (h d) -> p h d", h=4), ckvT[:, c, p * 128:(p + 1) * 128],
                               Wkv4[:, c, 4 * hh:4 * hh + 4, 0, :], c == 0, c == 1, [ckvT_d[p], Wkv], [bkk])
                        s_ = sq[p % 2]
                        act(s_[:], bkk[:], AF.Square, [bkk], [s_])
                        S.op("dve", lambda e, s_=s_, p=p: e.tensor_reduce(
                            out=ssk[:, p, :], in_=s_[:].rearrange("p (h d) -> p h d", h=4), axis=AX.X, op=ALU.add),
                            [s_], [ssk])
                        yield
                def kvB():
                    for hl in range(4):
                        h = 4 * hh + hl
                        for tc in range(8):
                            bk = banks[4 + (hl * 8 + tc) % 2]
                            for c in range(2):
                                mm(bk[:], Wkv[:, c, h * 256:h * 256 + 128], ckvT[:, c, tc * 512:(tc + 1) * 512],
                                   c == 0, c == 1, [Wkv] + ckvT_d[4 * tc:4 * tc + 4], [bk])
                            if (hl * 8 + tc) % 2:
                                act(KTn[:, hl, tc * 512:(tc + 1) * 512], bk[:], AF.Copy, [bk, gk_col], [KTn_d[hl][tc]],
                                    scale=gk_col[:, 0:1])
                            else:
                                ts(KTn[:, hl, tc * 512:(tc + 1) * 512], bk[:], gk_col[:, 0:1], None, ALU.mult, None,
                                   [bk, gk_col], [KTn_d[hl][tc]])
                            yield
                zip_gens(kvA(), kvB())
                tt(ksc[:], ssk[:], ssr[:].unsqueeze(2).to_broadcast([128, NB, 4]), ALU.add, [ssk, ssr], [ksc])
                act(ksc[:], ksc[:], AF.Ln, [ksc, epsc], [ksc], bias=epsc[:, 0:1], scale=1.0 / 192.0)
                act(ksc[:], ksc[:], AF.Exp, [ksc, lnsc], [ksc], bias=lnsc[:, 0:1], scale=-0.5)
                if hh == 0:
                    for i in range(4):
                        qbuild(0, 0, i, 0)
                for g in range(NGRP):
                    si = hh * NGRP + g
                    for hl in range(4):
                        attend(hh, g, hl, si % 2)
                        if si + 1 < len(seq):
                            qbuild(seq[si + 1][0], seq[si + 1][1], hl, (si + 1) % 2)
            if debug:
                S.barrier()
                d5 = dbg_out("d_OT", [128, 8, NOWN * 128], BF16)
                S.dma("sp", d5, OT, OT_d, [])
        A.pop()
        A.pop()
        S.barrier()

        if stage >= 3:
            A.push()
            cp_pref[0] = "act"
            tbanks[:] = [6, 7]
            NOPOOL = ("pe", "act", "dve", "sp")
            Gbuf = A.tile([128, D], F32)
            invR = A.tile([128, 128], F32)
            bload(invR, c_invr)
            Wple = A.tile([128, 2, D], BF16)
            S.dma("pool", Wple[:], w_ple.rearrange("(c p) n -> p c n", p=128), [], [Wple])
            state = [A.tile([128, 512], F32) for _ in range(8)]
            state_bf = [A.tile([128, 512], BF16) for _ in range(2)]
            for s_ in state:
                memset(s_[:], 0.0, [s_])
            slots = [A.tile([128, 8, 512], BF16) for _ in range(NSLOT)]
            sl_i = [0]

            def wslot(w_ap, k0, nk, c0, ncols=512):
                sl = slots[sl_i[0] % NSLOT]
                sl_i[0] += 1
                src = w_ap[k0 * 128:(k0 + nk) * 128, c0:c0 + ncols].rearrange("(c p) n -> p c n", p=128)
                S.dma("pool", sl[:, 0:nk, 0:ncols], src, [], [sl])
                return sl

            xown = [A.tile([128, D], F32) for _ in range(4)]
            junk = A.tile([128, D], BF16)
            xnb = [A.tile([128, D], BF16) for _ in range(2)]
            xnTo = [A.tile([128, 8, 128], BF16) for _ in range(4)]
            stt3 = [A.tile([128, 8], F32) for _ in range(2)]
            ost = [A.tile([128, 8], F32) for _ in range(2)]
            roT = A.view([128, 16, 512], BF16)
            roT_d = [Dep() for _ in range(4)]
            pbf = [A.tile([128, 256], BF16) for _ in range(2)]
            gam = [1.0 - 2.0 ** (-5 - h) for h in range(4)]
            gamC = [g_ ** 128 for g_ in gam]

            def proj_tm(w_ap, K, c0, ncols, lhs_fn, blocks, evac, bank_of):
                nkp = max(K // 1024, 1)
                kper = min(K // 128, 8)
                bks = {}
                for kp in range(nkp):
                    sl = wslot(w_ap, kp * 8, kper, c0, ncols)
                    for bi, b in enumerate(blocks):
                        if kp == 0:
                            bks[b] = banks[bank_of(bi)]
                        bk = bks[b]
                        for c in range(kper):
                            lap, ldeps = lhs_fn(b, kp * 8 + c)
                            mm(bk[:, 0:ncols], lap, sl[:, c, 0:ncols], kp == 0 and c == 0,
                               kp == nkp - 1 and c == kper - 1, list(ldeps) + [sl], [bk])
                        if kp == nkp - 1:
                            evac(b, bk)

            def sigmoid_from(bk, ncols, dst_t):
                act(dst_t[:, 0:ncols], bk[:, 0:ncols], AF.Exp, [bk], [dst_t], scale=-1.0)
                act(dst_t[:, 0:ncols], dst_t[:, 0:ncols], AF.Ln, [dst_t, onec], [dst_t], bias=onec[:, 0:1], scale=1.0)
                act(dst_t[:, 0:ncols], dst_t[:, 0:ncols], AF.Exp, [dst_t], [dst_t], scale=-1.0)

            def norm_to_T(b, G_src, dstT, dstT_d):
                st = stt3[b % 2]
                xn_ = xnb[b % 2]
                act(junk[:], xown[b][:], AF.Square, [xown[b]], [junk, st], scale=1.0 / 32.0, accum_out=st[:, 0:1])
                rstd_from_ms(st[:, 0:1], st[:, 0:1], [st], [st])
                stt(xn_[:], xown[b][:], st[:, 0:1], G_src[:], ALU.mult, ALU.mult, [xown[b], st, G_src], [xn_])
                transpose_to(dstT[:, :, b * 128:(b + 1) * 128], [xn_[:, c * 128:(c + 1) * 128] for c in range(8)],
                             128, 8, [xn_], [dstT_d[b]])

            for g in range(NGRP):
                S.barrier(NOPOOL)
                A.push()
                xnTt = [A.tile([128, 8, 128], BF16) for _ in range(4)]
                csR = A.tile([128, 2 * 8 * 128], F32)
                sinR = csR.ap[:, 0:1024].rearrange("p (a b) -> p a b", a=8)
                cosR = csR.ap[:, 1024:2048].rearrange("p (a b) -> p a b", a=8)
                A.push()
                xoth = [A.tile([128, D], F32) for _ in range(4)]
                xn8 = [A.tile([128, D], BF16) for _ in range(8)]
                kfR = A.tile([128, 2048], F32)
                kiR = A.tile([128, 2048], I32)
                bload(Gbuf, g_mix)
                for half in range(2):
                    o3 = csR.ap[:, half * 1024:(half + 1) * 1024].rearrange("p (a b) -> p a b", a=8)
                    tt(o3, post_f[:, 8 * g:8 * g + 8].unsqueeze(2).to_broadcast([128, 8, 128]),
                       invR[:].unsqueeze(1).to_broadcast([128, 8, 128]), ALU.mult, [post_f, invR], [csR])
                sincos(csR, kfR, kiR, [], 1024, do_sin=False)
                items = []
                for i in range(4):
                    items.append((xoth[i], xn8[2 * i], xnTt[i], xt[4 * g + i]))
                    items.append((xown[i], xn8[2 * i + 1], xnTo[i], xo[4 * g + i]))
                norm_batch(items, Gbuf, stt3[0], 0)
                act(csR[:], csR[:], AF.Sin, [csR], [csR])
                A.pop()
                S.barrier(NOPOOL)

                A.push()
                HB = []
                for _ in range(2):
                    HB.append(dict(
                        Kh=[A.tile([128, 256], BF16) for _ in range(8)],
                        Vv=[A.tile([128, 512], BF16) for _ in range(8)],
                        Qt=[A.tile([128, 256], BF16) for _ in range(4)],
                        Kt=[A.tile([128, 256], BF16) for _ in range(4)],
                        SG=[A.tile([128, 512], F32) for _ in range(4)]))
                Gr1 = A.tile([128, 512], F32)
                rA = [A.tile([128, 256], F32)] * 2
                rB = [A.tile([128, 256], F32)] * 2
                rC = [A.tile([128, 256], F32) for _ in range(2)]
                sig = [A.tile([128, 512], F32)] * 2
                QT = [A.tile([128, 2, 128], BF16) for _ in range(2)]
                KT = [A.tile([128, 2, 128], BF16) for _ in range(2)]
                scb = [A.tile([128, 128], BF16) for _ in range(2)]
                rog = [A.tile([128, 512], BF16) for _ in range(2)]

                def lhs_x(b, k):
                    i, own = b
                    t_ = (xnTo if own else xnTt)[i]
                    return t_[:, k, :], [t_]

                all8 = [(i, own) for i in range(4) for own in (0, 1)]
                own4 = [(i, 1) for i in range(4)]

                def proj_gen(h):
                    hb = HB[h % 2]
                    Kh, Vv, Qt, Kt, SG, Gr = hb["Kh"], hb["Vv"], hb["Qt"], hb["Kt"], hb["SG"], Gr1
                    bload(Gr, g_ret[0:1, h * 512:(h + 1) * 512])

                    def rope_evac(kind):
                        def ev(b, bk):
                            i, own = b
                            blk = 2 * i + own
                            a_, b_, c_ = rA[blk % 2], rB[blk % 2], rC[blk % 2]
                            x3 = bk[:, 0:256].rearrange("p (a f) -> p a f", a=2)
                            cb = cosR[:, blk, :].unsqueeze(1).to_broadcast([128, 2, 128])
                            sb_ = sinR[:, blk, :].unsqueeze(1).to_broadcast([128, 2, 128])
                            dcol = dec[:, h:h + 1] if kind == "q" else dec[:, 8 + h:9 + h]
                            stt(a_[:].rearrange("p (a f) -> p a f", a=2), x3, dcol, cb, ALU.mult, ALU.mult,
                                [bk, csR, dec], [a_])
                            stt(b_[:].rearrange("p (a f) -> p a f", a=2), x3, dcol, sb_, ALU.mult, ALU.mult,
                                [bk, csR, dec], [b_])
                            if kind == "q":
                                tt(Qt[i][:, 0:128], a_[:, 0:128], b_[:, 128:256], ALU.subtract, [a_, b_], [Qt[i]])
                                tt(Qt[i][:, 128:256], a_[:, 128:256], b_[:, 0:128], ALU.add, [a_, b_], [Qt[i]])
                            elif not own:
                                tt(Kh[blk][:, 0:128], a_[:, 0:128], b_[:, 128:256], ALU.subtract, [a_, b_], [Kh[blk]])
                                tt(Kh[blk][:, 128:256], a_[:, 128:256], b_[:, 0:128], ALU.add, [a_, b_], [Kh[blk]])
                            else:
                                tt(c_[:, 0:128], a_[:, 0:128], b_[:, 128:256], ALU.subtract, [a_, b_], [c_])
                                tt(c_[:, 128:256], a_[:, 128:256], b_[:, 0:128], ALU.add, [a_, b_], [c_])
                                act(Kh[blk][:], c_[:], AF.Copy, [c_], [Kh[blk]])
                                act(Kt[i][:], c_[:], AF.Copy, [c_], [Kt[i]], scale=float(gam[h] ** -128))
                        return ev

                    def v_evac(b, bk):
                        i, own = b
                        cp_alt(Vv[2 * i + own][:], bk[:], [bk], [Vv[2 * i + own]])

                    def g_evac(b, bk):
                        i, own = b
                        sg = sig[i % 2]
                        sigmoid_from(bk, 512, sg)
                        tt(sg[:], sg[:], bk[:], ALU.mult, [sg, bk], [sg])
                        tt(SG[i][:], sg[:], Gr[:], ALU.mult, [sg, Gr], [SG[i]])

                    for (c0, ncols, blks, ev) in ((OFF_RK + h * 256, 256, all8, rope_evac("k")),
                                                  (OFF_RQ + h * 256, 256, own4, rope_evac("q")),
                                                  (OFF_RV + h * 512, 512, all8, v_evac),
                                                  (OFF_RG + h * 512, 512, own4, g_evac)):
                        sl = wslot(w_in, 0, 8, c0, ncols)
                        for bi, b_ in enumerate(blks):
                            bk = banks[bi % 4]
                            for c in range(8):
                                lap, ldeps = lhs_x(b_, c)
                                mm(bk[:, 0:ncols], lap, sl[:, c, 0:ncols], c == 0, c == 7, list(ldeps) + [sl], [bk])
                            ev(b_, bk)
                            yield

                def ret_gen(h):
                    hb = HB[h % 2]
                    Kh, Vv, Qt, Kt, SG = hb["Kh"], hb["Vv"], hb["Qt"], hb["Kt"], hb["SG"]

                    def state_update(blk):
                        for c in range(2):
                            bk = banks[4 + c]
                            mm(bk[:], Kh[blk][:, c * 128:(c + 1) * 128], Vv[blk][:], True, True,
                               [Kh[blk], Vv[blk]], [bk])
                            s_ = state[h * 2 + c]
                            stt(s_[:], s_[:], gamC[h], bk[:], ALU.mult, ALU.add, [s_, bk], [s_])

                    for i in range(4):
                        state_update(2 * i)
                        blk = 2 * i + 1
                        for c in range(2):
                            cp_alt(state_bf[c][:], state[2 * h + c][:], [state[2 * h + c]], [state_bf[c]])
                        qT_, kT_ = QT[i % 2], KT[i % 2]
                        transpose_to(qT_[:], [Qt[i][:, c * 128:(c + 1) * 128] for c in range(2)], 128, 2, [Qt[i]], [qT_])
                        transpose_to(kT_[:], [Kt[i][:, c * 128:(c + 1) * 128] for c in range(2)], 128, 2, [Kt[i]], [kT_])
                        yield 2
                        bs = banks[4]
                        for c in range(2):
                            mm(bs[:, 0:128], kT_[:, c, :], qT_[:, c, :], c == 0, c == 1, [kT_, qT_], [bs])
                        sc = scb[i % 2]
                        tt(sc[:], bs[:, 0:128], tri[:], ALU.mult, [bs, tri], [sc])
                        bo = banks[5]
                        mm(bo[:], sc[:], Vv[blk][:], True, False, [sc, Vv[blk]], [bo])
                        for c in range(2):
                            mm(bo[:], qT_[:, c, :], state_bf[c][:], False, c == 1, [qT_, state_bf[c]], [bo])
                        ot = ost[i % 2]
                        act(junk[:, 0:512], bo[:], AF.Square, [bo], [junk, ot], scale=1.0 / math.sqrt(512.0),
                            accum_out=ot[:, 0:1])
                        rstd_from_ms(ot[:, 0:1], ot[:, 0:1], [ot], [ot])
                        rg_ = rog[i % 2]
                        stt(rg_[:], bo[:], ot[:, 0:1], SG[i][:], ALU.mult, ALU.mult, [bo, ot, SG[i]], [rg_])
                        yield 2
                        transpose_to(roT[:, 4 * h:4 * h + 4, i * 128:(i + 1) * 128],
                                     [rg_[:, c * 128:(c + 1) * 128] for c in range(4)], 128, 4, [rg_], [roT_d[i]])
                        yield 1
                        state_update(blk)
                        yield 1

                for _ in proj_gen(0):
                    pass
                for h in range(4):
                    pg_it = proj_gen(h + 1) if h < 3 else None
                    for n_ in ret_gen(h):
                        for _ in range(n_):
                            if pg_it is not None:
                                try:
                                    next(pg_it)
                                except StopIteration:
                                    pg_it = None
                    if pg_it is not None:
                        for _ in pg_it:
                            pass
                A.pop()
                A.pop()
                S.barrier(NOPOOL)

                A.push()
                mixed = [A.tile([128, D], F32) for _ in range(4)]
                mixb = [A.tile([128, D], BF16) for _ in range(2)]
                mixT = A.view([128, 8, 512], BF16)
                mixT_d = [Dep() for _ in range(4)]
                sig = [A.tile([128, 512], F32) for _ in range(2)]
                h2T = A.view([128, 8, 512], BF16)
                h2T_d = [Dep() for _ in range(4)]
                actT = A.view([128, 16, 512], BF16)
                actT_d = [Dep() for _ in range(16)]
                pT = [A.tile([128, 2, 128], BF16) for _ in range(4)]
                rl = [A.tile([128, 512], F32) for _ in range(2)]
                xn4 = [A.tile([128, D], BF16) for _ in range(4)]

                def lhs_ro(b, k):
                    return roT[:, k, b * 128:(b + 1) * 128], [roT_d[b]]

                def lhs_ot(b, k):
                    return OT[:, k, (4 * g + b) * 128:(4 * g + b + 1) * 128], [OT_d[g]]

                def lhs_xo(b, k):
                    return xnTo[b][:, k, :], [xnTo[b]]

                for (wbr, Kbr, lhs_br, goff, first) in ((w_br, 2048, lhs_ro, OFF_GR, True),
                                                        (w_bm, 1024, lhs_ot, OFF_GM, False)):
                    for ct in range(2):
                        held = {}

                        def hold(b, bk, held=held):
                            held[b] = bk
                        proj_tm(wbr, Kbr, ct * 512, 512, lhs_br, [0, 1, 2, 3], hold, lambda bi: bi)

                        def gate_evac(b, bk, ct=ct, first=first, held=held):
                            sg = sig[b % 2]
                            sigmoid_from(bk, 512, sg)
                            mx = mixed[b]
                            if first:
                                tt(mx[:, ct * 512:(ct + 1) * 512], sg[:], held[b][:], ALU.mult, [sg, held[b]], [mx])
                            else:
                                tt(sg[:], sg[:], held[b][:], ALU.mult, [sg, held[b]], [sg])
                                tt(mx[:, ct * 512:(ct + 1) * 512], mx[:, ct * 512:(ct + 1) * 512], sg[:], ALU.add,
                                   [mx, sg], [mx])
                        proj_tm(w_in, 1024, goff + ct * 512, 512, lhs_xo, [0, 1, 2, 3], gate_evac,
                                lambda bi: 4 + bi % 2)
                for b in range(4):
                    cp("act" if b % 2 else "dve", xn4[b][:], mixed[b][:], [mixed[b]], [xn4[b]])
                for b in range(4):
                    transpose_to(mixT[:, :, b * 128:(b + 1) * 128], [xn4[b][:, c * 128:(c + 1) * 128] for c in range(8)],
                                 128, 8, [xn4[b]], [mixT_d[b]])

                def lhs_mix(b, k):
                    return mixT[:, k, b * 128:(b + 1) * 128], [mixT_d[b]]

                def res_evac(ct):
                    def ev(b, bk):
                        tt(xown[b][:, ct * 512:(ct + 1) * 512], xown[b][:, ct * 512:(ct + 1) * 512], bk[:], ALU.add,
                           [xown[b], bk], [xown[b]])
                    return ev

                for ct in range(2):
                    proj_tm(w_o, 1024, ct * 512, 512, lhs_mix, [0, 1, 2, 3], res_evac(ct), lambda bi: bi % 6)
                bload(Gbuf, g_mlp)
                norm_batch([(xown[b], xn4[b], T(h2T[:, :, b * 128:(b + 1) * 128], h2T_d[b]), None) for b in range(4)],
                           Gbuf, stt3[0], 0)

                def lhs_act(b, k):
                    return actT[:, k, b * 128:(b + 1) * 128], [actT_d[k]]

                for hf in range(2):
                    for ft in range(4):
                        sl = wslot(w_up, 0, 8, hf * 2048 + ft * 512, 512)
                        for fi in range(4):
                            f = ft * 4 + fi
                            bk = banks[4 + f % 2]
                            for c in range(8):
                                mm(bk[:], sl[:, c, fi * 128:(fi + 1) * 128], h2T[:, c, :], c == 0, c == 7,
                                   [sl] + h2T_d, [bk])
                            r_ = rl[f % 2]
                            act(r_[:], bk[:], AF.Relu, [bk], [r_])
                            act(actT[:, f, :], r_[:], AF.Square, [r_], [actT_d[f]])
                    for ct in range(2):
                        proj_tm(w_down[hf * 2048:(hf + 1) * 2048, :], 2048, ct * 512, 512, lhs_act, [0, 1, 2, 3],
                                res_evac(ct), lambda bi: bi)

                bload(Gbuf, g_ple)
                norm_batch([(xown[b], xn4[b], T(h2T[:, :, b * 128:(b + 1) * 128], h2T_d[b]), None) for b in range(4)],
                           Gbuf, stt3[1], 0)
                for b in range(4):
                    pb_ = pbf[b % 2]
                    S.dma("pool", pb_[:], pp[4 * g + b], [], [pb_])
                    transpose_to(pT[b][:], [pb_[:, c * 128:(c + 1) * 128] for c in range(2)], 128, 2, [pb_], [pT[b]])

                def lhs_h3(b, k):
                    return h2T[:, k, b * 128:(b + 1) * 128], [h2T_d[b]]

                for ct in range(2):
                    def ple_evac(b, bk, ct=ct):
                        sg = sig[b % 2]
                        sigmoid_from(bk, 512, sg)
                        bp = banks[4 + b % 2]
                        for c in range(2):
                            mm(bp[:], pT[b][:, c, :], Wple[:, c, ct * 512:(ct + 1) * 512], c == 0, c == 1,
                               [pT[b], Wple], [bp])
                        tt(sg[:], sg[:], bp[:], ALU.mult, [sg, bp], [sg])
                        tt(xown[b][:, ct * 512:(ct + 1) * 512], xown[b][:, ct * 512:(ct + 1) * 512], sg[:], ALU.add,
                           [xown[b], sg], [xown[b]])
                    proj_tm(w_pg, 1024, ct * 512, 512, lhs_h3, [0, 1, 2, 3], ple_evac, lambda bi: bi)
                for b in range(4):
                    S.dma("sp", out_d[4 * g + b], xown[b][:], [xown[b]], [])
                A.pop()
            A.pop()

        S.barrier()
        S.emit()
    return nc, dbg


def make_in_maps(x, p, positions, norm_mix, w_in, ret_norm, q_lat_norm, kv_lat_norm, w_uq, w_ukv,
                 q_norm, k_norm, w_br, w_bm, w_o, norm_mlp, w_up, w_down, norm_ple, w_ple_gate, w_ple):
    f32 = np.float32
    x = np.asarray(x, f32)
    p = np.asarray(p, f32)
    positions = np.asarray(positions, np.int32)
    shared = {
        "w_in": np.ascontiguousarray(np.asarray(w_in, f32)[0]),
        "w_uq": np.ascontiguousarray(np.asarray(w_uq, f32)[0]),
        "w_ukv": np.ascontiguousarray(np.asarray(w_ukv, f32)[0]),
        "w_br": np.ascontiguousarray(np.asarray(w_br, f32)[0]),
        "w_bm": np.ascontiguousarray(np.asarray(w_bm, f32)[0]),
        "w_o": np.ascontiguousarray(np.asarray(w_o, f32)[0]),
        "w_up": np.ascontiguousarray(np.asarray(w_up, f32)[0]),
        "w_down": np.ascontiguousarray(np.asarray(w_down, f32)[0]),
        "w_pg": np.ascontiguousarray(np.asarray(w_ple_gate, f32)[0]),
        "w_ple": np.ascontiguousarray(np.asarray(w_ple, f32)[0]),
        "g_mix": np.asarray(norm_mix, f32).reshape(1, -1),
        "g_ret": np.asarray(ret_norm, f32).reshape(1, -1),
        "g_ql": np.asarray(q_lat_norm, f32).reshape(1, -1),
        "g_kl": np.asarray(kv_lat_norm, f32).reshape(1, -1),
        "g_q": np.asarray(q_norm, f32).reshape(1, -1),
        "g_k": np.asarray(k_norm, f32).reshape(1, -1),
        "g_mlp": np.asarray(norm_mlp, f32).reshape(1, -1),
        "g_ple": np.asarray(norm_ple, f32).reshape(1, -1),
    }
    shared["c_ident"] = np.eye(128, dtype=f32)
    kk = np.arange(128)
    shared["c_tri"] = (kk[:, None] <= kk[None, :]).astype(f32)
    shared["c_invm"] = (f32(10000.0) ** (-np.arange(32, dtype=f32) / f32(32))).astype(f32).reshape(1, 32)
    shared["c_invr"] = (f32(10000.0) ** (-np.arange(128, dtype=f32) / f32(128))).astype(f32).reshape(1, 128)
    t = np.arange(128, dtype=np.float64)
    dec = np.zeros((128, 12), np.float64)
    for h in range(4):
        gmm = 1.0 - 2.0 ** (-5 - h)
        dec[:, h] = gmm ** (t + 1.0)
        dec[:, 4 + h] = gmm ** (-(t + 1.0)) / 16.0
        dec[:, 8 + h] = gmm ** (127.0 - t) / 16.0
    shared["c_dec"] = dec.astype(f32)
    maps = []
    for c in range(8):
        b, j = c // 2, c % 2
        xb = x[b].reshape(NB, 128, D)
        pb = positions[b].reshape(NB, 128)
        xo_ = np.ascontiguousarray(xb[j::2])
        po = pb[j::2]
        kb = np.zeros((128, NB), f32)
        if j == 1:
            xt_ = np.ascontiguousarray(xb[0::2])
            pt = pb[0::2]
        else:
            xt_ = np.concatenate([np.zeros((1, 128, D), f32), xb[1:NB - 1:2]], axis=0)
            pt = np.concatenate([np.zeros((1, 128), np.int32), pb[1:NB - 1:2]], axis=0)
            kb[:, 0] = -30000.0
        post = np.zeros((128, NB), np.int32)
        post[:, 0::2] = pt.T
        post[:, 1::2] = po.T
        m = dict(shared)
        m["xo"] = xo_
        m["xt"] = np.ascontiguousarray(xt_)
        m["pp"] = np.ascontiguousarray(p[0, b].reshape(NB, 128, 256)[j::2])
        m["post"] = post
        m["kbias"] = kb
        maps.append(m)
    return maps


_CACHE = {}


def kernel(**inputs):
    if "nc" not in _CACHE:
        _CACHE["nc"] = build_program()[0]
    nc = _CACHE["nc"]
    maps = make_in_maps(**inputs)
    res = run_bass_kernel_spmd(nc, maps, core_ids=list(range(8)))
    out = np.zeros((4, NB, 128, D), np.float32)
    for c in range(8):
        b, j = c // 2, c % 2
        out[b, j::2] = np.asarray(res.results[c]["out"]).reshape(NOWN, 128, D)
    return out.reshape(4, NB * 128, D)
```
